# Optimizing a Trainium2 kernel written in Bass

```python
import math
import jax, jax.numpy as jnp
from jax import lax
import numpy as np

D_MODEL = 1024
BATCH = 32
SEQ = 2048
DEPTH = 2

HEAD_DIM = 64
A_HEADS = 6
IDX_HEADS = 8
IDX_DIM = 64
A_TOPK_MAX = 256
A_Q_BLOCK = 128
B_SLOTS = 4
B_PATTERNS = ((128, 1), (512, 4), (2048, 16))
B_GROUPS = 3
B_PAD = 2048
B_Q_BLOCK = 64
C_HEADS = 6
C_BLOCK = 256
C_TOPK = 3
C_Q_CHUNK = 32
N_BUCKETS = 32
MAX_DISTANCE = 2048
N_BIAS_HEADS = A_HEADS + B_GROUPS * B_SLOTS + C_HEADS
D_FF = 4 * D_MODEL
PLE_DIM = 256
N_BRANCH = 3
ALPHA = (2 * DEPTH) ** 0.25
BETA = (8 * DEPTH) ** -0.25
LN_EPS = 1e-5
NEG = -1e30

IN_WIDTHS = (
    A_HEADS * HEAD_DIM, HEAD_DIM, HEAD_DIM,
    IDX_HEADS * IDX_DIM, IDX_DIM, IDX_HEADS,
    B_GROUPS * B_SLOTS * HEAD_DIM, B_SLOTS * HEAD_DIM, B_SLOTS * HEAD_DIM,
    C_HEADS * HEAD_DIM, C_HEADS * HEAD_DIM, C_HEADS * HEAD_DIM,
)
D_IN = sum(IN_WIDTHS)
SPLIT_POINTS = tuple(int(c) for c in np.cumsum(IN_WIDTHS)[:-1])

kernel_name = "hybrid_dsa_dilated_moba_deepnorm"


def layer_norm(x, g, b):
    xf = x.astype(jnp.float32)
    mu = jnp.mean(xf, axis=-1, keepdims=True)
    var = jnp.mean(jnp.square(xf - mu), axis=-1, keepdims=True)
    return ((xf - mu) * lax.rsqrt(var + LN_EPS) * g + b).astype(x.dtype)


def rel_bucket(dist):
    n = jnp.maximum(dist, 0)
    max_exact = N_BUCKETS // 2
    nf = jnp.maximum(n, 1).astype(jnp.float32)
    large = max_exact + (jnp.log(nf / max_exact) / math.log(MAX_DISTANCE / max_exact)
                         * (N_BUCKETS - max_exact)).astype(jnp.int32)
    large = jnp.minimum(large, N_BUCKETS - 1)
    return jnp.where(n < max_exact, n, large)


def masked_softmax(logits, mask):
    return jax.nn.softmax(jnp.where(mask, logits.astype(jnp.float32), NEG), axis=-1)


def dsa_mixer(q, k, v, iq, ik, iw, bias_tab):
    bsz, s_len = q.shape[0], q.shape[1]
    topk = min(A_TOPK_MAX, s_len // 4)
    scale = HEAD_DIM ** -0.5
    iscale = IDX_DIM ** -0.5
    kpos = jnp.arange(s_len)
    qloc = jnp.arange(A_Q_BLOCK)

    def block(t0):
        qb = lax.dynamic_slice_in_dim(q, t0, A_Q_BLOCK, axis=1)
        iqb = lax.dynamic_slice_in_dim(iq, t0, A_Q_BLOCK, axis=1)
        iwb = lax.dynamic_slice_in_dim(iw, t0, A_Q_BLOCK, axis=1)
        tpos = t0 + qloc
        sc = jnp.einsum('bqhd,bsd->bqhs', iqb, ik) * iscale
        index = jnp.einsum('bqh,bqhs->bqs', iwb, jax.nn.relu(sc)).astype(jnp.float32)
        causal = kpos[None, :] <= tpos[:, None]
        index = jnp.where(causal[None], index, -jnp.inf)
        _, sel = lax.top_k(index, topk)
        ksel = jax.vmap(lambda kb, ib: kb[ib])(k, sel)
        vsel = jax.vmap(lambda vb, ib: vb[ib])(v, sel)
        dist = tpos[None, :, None] - sel
        bias = jnp.moveaxis(bias_tab[rel_bucket(dist)], -1, 2)
        logits = jnp.einsum('bqhd,bqkd->bqhk', qb, ksel).astype(jnp.float32) * scale + bias
        probs = masked_softmax(logits, (dist >= 0)[:, :, None, :])
        return jnp.einsum('bqhk,bqkd->bqhd', probs.astype(v.dtype), vsel)

    starts = jnp.arange(s_len // A_Q_BLOCK) * A_Q_BLOCK
    out = lax.map(block, starts)
    return jnp.moveaxis(out, 0, 1).reshape(bsz, s_len, A_HEADS * HEAD_DIM)


def dilated_mixer(q, k, v, bias_tab):
    bsz, s_len = q.shape[0], q.shape[1]
    scale = HEAD_DIM ** -0.5
    kp = jnp.pad(k, ((0, 0), (B_PAD, 0), (0, 0), (0, 0)))
    vp = jnp.pad(v, ((0, 0), (B_PAD, 0), (0, 0), (0, 0)))
    qloc = jnp.arange(B_Q_BLOCK)

    def block(t0):
        qb = lax.dynamic_slice_in_dim(q, t0, B_Q_BLOCK, axis=1)
        tpos = t0 + qloc
        maxes, denoms, outs = [], [], []
        for g, (win, dil) in enumerate(B_PATTERNS):
            offs = jnp.arange(win // dil + 1) * dil
            kc = lax.dynamic_slice_in_dim(kp, t0 + B_PAD - win, B_Q_BLOCK + win, axis=1)
            vc = lax.dynamic_slice_in_dim(vp, t0 + B_PAD - win, B_Q_BLOCK + win, axis=1)
            lidx = qloc[:, None] + win - offs[None, :]
            kg = kc[:, lidx]
            vg = vc[:, lidx]
            bias = bias_tab[rel_bucket(offs), g * B_SLOTS:(g + 1) * B_SLOTS].T
            logits = jnp.einsum('bqhd,bqjhd->bqhj', qb[:, :, g], kg).astype(jnp.float32) * scale + bias
            valid = (tpos[:, None] - offs[None, :]) >= 0
            logits = jnp.where(valid[None, :, None, :], logits, NEG)
            m = jnp.max(logits, axis=-1, keepdims=True)
            e = jnp.exp(logits - m)
            den = jnp.sum(e, axis=-1, keepdims=True)
            outs.append(jnp.einsum('bqhj,bqjhd->bqhd', (e / den).astype(v.dtype), vg))
            maxes.append(m)
            denoms.append(den)
        m_all = jnp.stack(maxes)
        wts = jnp.stack(denoms) * jnp.exp(m_all - jnp.max(m_all, axis=0, keepdims=True))
        wts = wts / jnp.sum(wts, axis=0, keepdims=True)
        return jnp.sum(wts.astype(v.dtype) * jnp.stack(outs), axis=0)

    starts = jnp.arange(s_len // B_Q_BLOCK) * B_Q_BLOCK
    out = lax.map(block, starts)
    return jnp.moveaxis(out, 0, 1).reshape(bsz, s_len, B_SLOTS * HEAD_DIM)


def moba_mixer(q, k, v, bias_tab):
    bsz, s_len, n_h, dh = q.shape
    scale = dh ** -0.5
    nblk = -(-s_len // C_BLOCK)
    pad = nblk * C_BLOCK - s_len
    kp = jnp.pad(k, ((0, 0), (0, pad), (0, 0), (0, 0)))
    vp = jnp.pad(v, ((0, 0), (0, pad), (0, 0), (0, 0)))
    kb = kp.reshape(bsz, nblk, C_BLOCK, n_h, dh)
    k_mean = jnp.mean(kb, axis=2)
    kbh = jnp.moveaxis(kb, 3, 1)
    vbh = jnp.moveaxis(vp.reshape(bsz, nblk, C_BLOCK, n_h, dh), 3, 1)
    ntop = min(C_TOPK, nblk)
    bi = jnp.arange(bsz)[:, None, None, None]
    hi = jnp.arange(n_h)[None, None, :, None]
    tab_h = bias_tab.T
    kin = jnp.arange(C_BLOCK)
    qloc = jnp.arange(C_Q_CHUNK)

    def chunk(t0):
        qc = lax.dynamic_slice_in_dim(q, t0, C_Q_CHUNK, axis=1)
        tpos = t0 + qloc
        cur = t0 // C_BLOCK
        gate = jnp.einsum('bqhd,bnhd->bqhn', qc, k_mean).astype(jnp.float32)
        gate = jnp.where(jnp.arange(nblk) < cur, gate, -jnp.inf)
        _, sel = lax.top_k(gate, ntop)
        ksel = kbh[bi, hi, sel]
        vsel = vbh[bi, hi, sel]
        dist_sel = tpos[None, :, None, None, None] - (sel[..., None] * C_BLOCK + kin)
        bias_sel = tab_h[hi[..., None], rel_bucket(dist_sel)]
        l_sel = jnp.einsum('bqhd,bqhnkd->bqhnk', qc, ksel).astype(jnp.float32) * scale + bias_sel
        valid_sel = jnp.broadcast_to((sel < cur)[..., None], l_sel.shape)
        kown = lax.dynamic_slice_in_dim(kp, cur * C_BLOCK, C_BLOCK, axis=1)
        vown = lax.dynamic_slice_in_dim(vp, cur * C_BLOCK, C_BLOCK, axis=1)
        dist_own = tpos[:, None] - (cur * C_BLOCK + kin)[None, :]
        bias_own = jnp.moveaxis(bias_tab[rel_bucket(dist_own)], -1, 1)[None]
        l_own = jnp.einsum('bqhd,bkhd->bqhk', qc, kown).astype(jnp.float32) * scale + bias_own
        valid_own = jnp.broadcast_to((dist_own >= 0)[None, :, None, :], l_own.shape)
        n_sel = ntop * C_BLOCK
        logits = jnp.concatenate([l_sel.reshape(bsz, C_Q_CHUNK, n_h, n_sel), l_own], axis=-1)
        mask = jnp.concatenate([valid_sel.reshape(bsz, C_Q_CHUNK, n_h, n_sel), valid_own], axis=-1)
        probs = masked_softmax(logits, mask).astype(v.dtype)
        p_sel = probs[..., :n_sel].reshape(bsz, C_Q_CHUNK, n_h, ntop, C_BLOCK)
        p_own = probs[..., n_sel:]
        return (jnp.einsum('bqhnk,bqhnkd->bqhd', p_sel, vsel)
                + jnp.einsum('bqhk,bkhd->bqhd', p_own, vown))

    starts = jnp.arange(s_len // C_Q_CHUNK) * C_Q_CHUNK
    out = lax.map(chunk, starts)
    return jnp.moveaxis(out, 0, 1).reshape(bsz, s_len, n_h * dh)


def setup_inputs(seed: int = 0) -> dict:
    key = jax.random.key(seed)
    ks = jax.random.split(key, 20)

    def nrm(k, shape, fan_in, gain=1.0):
        return jax.random.normal(k, shape, jnp.float32) * (gain * fan_in ** -0.5)

    def small(k, shape, s=0.02):
        return jax.random.normal(k, shape, jnp.float32) * s

    a_w = A_HEADS * HEAD_DIM
    b_w = B_SLOTS * HEAD_DIM
    c_w = C_HEADS * HEAD_DIM
    return {
        "x": jax.random.normal(ks[0], (BATCH, SEQ, D_MODEL), jnp.float32),
        "p": jax.random.normal(ks[1], (DEPTH, BATCH, SEQ, PLE_DIM), jnp.float32),
        "w_in": nrm(ks[2], (DEPTH, D_MODEL, D_IN), D_MODEL),
        "w_gate": nrm(ks[3], (DEPTH, D_MODEL, N_BRANCH * D_MODEL), D_MODEL),
        "w_br_a": nrm(ks[4], (DEPTH, a_w, D_MODEL), a_w, BETA),
        "w_br_b": nrm(ks[5], (DEPTH, b_w, D_MODEL), b_w, BETA),
        "w_br_c": nrm(ks[6], (DEPTH, c_w, D_MODEL), c_w, BETA),
        "w_out": nrm(ks[7], (DEPTH, D_MODEL, D_MODEL), D_MODEL, BETA),
        "ln1_g": 1.0 + small(ks[8], (DEPTH, D_MODEL)),
        "ln1_b": small(ks[9], (DEPTH, D_MODEL)),
        "w_up": nrm(ks[10], (DEPTH, D_MODEL, D_FF), D_MODEL, BETA),
        "w_down": nrm(ks[11], (DEPTH, D_FF, D_MODEL), D_FF, BETA),
        "w_ple_gate": nrm(ks[12], (DEPTH, D_MODEL, D_MODEL), D_MODEL),
        "w_ple": nrm(ks[13], (DEPTH, PLE_DIM, D_MODEL), PLE_DIM, BETA),
        "ln2_g": 1.0 + small(ks[14], (DEPTH, D_MODEL)),
        "ln2_b": small(ks[15], (DEPTH, D_MODEL)),
        "rel_bias": small(ks[16], (N_BUCKETS, N_BIAS_HEADS), 0.1),
    }


def reference(x, p, w_in, w_gate, w_br_a, w_br_b, w_br_c, w_out, ln1_g, ln1_b,
              w_up, w_down, w_ple_gate, w_ple, ln2_g, ln2_b, rel_bias):
    bsz, s_len, _ = x.shape
    bias_a = rel_bias[:, :A_HEADS]
    bias_b = rel_bias[:, A_HEADS:A_HEADS + B_GROUPS * B_SLOTS]
    bias_c = rel_bias[:, A_HEADS + B_GROUPS * B_SLOTS:]
    for i in range(DEPTH):
        proj = x @ w_in[i]
        aq, ak, av, iq, ik, iw, bq, bk, bv, cq, ck, cv = jnp.split(proj, SPLIT_POINTS, axis=-1)
        o_a = dsa_mixer(aq.reshape(bsz, s_len, A_HEADS, HEAD_DIM), ak, av,
                        iq.reshape(bsz, s_len, IDX_HEADS, IDX_DIM), ik, iw, bias_a)
        o_b = dilated_mixer(bq.reshape(bsz, s_len, B_GROUPS, B_SLOTS, HEAD_DIM),
                            bk.reshape(bsz, s_len, B_SLOTS, HEAD_DIM),
                            bv.reshape(bsz, s_len, B_SLOTS, HEAD_DIM), bias_b)
        o_c = moba_mixer(cq.reshape(bsz, s_len, C_HEADS, HEAD_DIM),
                         ck.reshape(bsz, s_len, C_HEADS, HEAD_DIM),
                         cv.reshape(bsz, s_len, C_HEADS, HEAD_DIM), bias_c)
        gates = jax.nn.sigmoid(x @ w_gate[i]).reshape(bsz, s_len, N_BRANCH, D_MODEL)
        merged = (gates[:, :, 0] * (o_a @ w_br_a[i])
                  + gates[:, :, 1] * (o_b @ w_br_b[i])
                  + gates[:, :, 2] * (o_c @ w_br_c[i]))
        x = layer_norm(ALPHA * x + merged @ w_out[i], ln1_g[i], ln1_b[i])
        h = jnp.square(jax.nn.relu(x @ w_up[i])) @ w_down[i]
        ple = jax.nn.sigmoid(x @ w_ple_gate[i]) * (p[i] @ w_ple[i])
        x = layer_norm(ALPHA * x + h + ple, ln2_g[i], ln2_b[i])
    return x
```

```python
import math
import numpy as np
import ml_dtypes
import concourse.bass as bass
import concourse.mybir as mybir
from concourse.bass_utils import run_bass_kernel_spmd

F32 = mybir.dt.float32
BF16 = mybir.dt.bfloat16
AF = mybir.ActivationFunctionType
ALU = mybir.AluOpType
AX = mybir.AxisListType

S = 2048
D = 1024
NT = 16
BIG = 30000.0
NIT = 18
FUSE_WAIT = True
ALPHA = 4 ** 0.25
EPS = 1e-5
WIC = 3656
FW = 2304
FWB = 512


class Buf:
    __slots__ = ("name", "w", "rs")
    ALL = []

    def __init__(self, name=""):
        Buf.ALL.append(self)
        self.name = name
        self.w = None
        self.rs = {}


class Stream:
    def __init__(self, sem):
        self.sem = sem
        self.count = 0


class Op:
    __slots__ = ("eng", "fn", "idx", "ewait", "swait", "stream", "epoch")

    def __init__(self, eng, fn):
        self.epoch = 0
        self.eng = eng
        self.fn = fn
        self.idx = 0
        self.ewait = {}
        self.swait = {}
        self.stream = None


class Prog:
    def __init__(self, nc):
        self.nc = nc
        self.e = {"pe": nc.tensor, "act": nc.scalar, "dve": nc.vector,
                  "pool": nc.gpsimd, "sp": nc.sync}
        self.ops = {k: [] for k in self.e}
        self.cnt = {k: 0 for k in self.e}
        self.sems = [{k: nc.alloc_semaphore("sem_" + k) for k in self.e}]
        self.epoch = 0
        self.streams = []

    def stream(self):
        s = Stream(self.nc.alloc_semaphore("dsem%d" % len(self.streams)))
        self.streams.append(s)
        return s

    INORDER = ("dve", "act")
    RAW_DIST = 3

    def _dep_op(self, o, d, raw=True):
        if d is None or d is o:
            return
        if d.stream is not None:
            s = d.stream
            o.swait[s] = max(o.swait.get(s, 0), s.count)
        else:
            if d.eng == "pe" and o.eng == "pe":
                return
            if d.eng == o.eng and d.eng in self.INORDER and d.epoch == o.epoch:
                if (not raw) or (o.idx - d.idx >= self.RAW_DIST):
                    return
            o.ewait[d.eng] = max(o.ewait.get(d.eng, 0), d.idx + 1)

    def op(self, eng, fn, r=(), w=(), stream=None):
        o = Op(eng, fn)
        o.epoch = self.epoch
        o.idx = self.cnt[eng]
        if stream is None:
            self.cnt[eng] += 1
        for b in r:
            self._dep_op(o, b.w)
        for b in w:
            self._dep_op(o, b.w, raw=False)
            for k, v in b.rs.items():
                if isinstance(k, Stream):
                    o.swait[k] = max(o.swait.get(k, 0), k.count)
                else:
                    if k == eng and (k == "pe" or k in self.INORDER):
                        continue
                    o.ewait[k] = max(o.ewait.get(k, 0), v)
        if stream is not None:
            o.stream = stream
        for b in r:
            if stream is not None:
                b.rs[stream] = 1
            else:
                b.rs[eng] = o.idx + 1
        for b in w:
            b.w = o
            b.rs = {}
        self.ops[eng].append(o)
        return o

    def dma(self, stream, out, in_, r=(), w=(), new_batch=True, eng="sp"):
        o = self.op(eng, lambda e: e.dma_start(out=out, in_=in_), r, w, stream=stream)
        if new_batch and stream.count > 0:
            o.swait[stream] = max(o.swait.get(stream, 0), stream.count)
        stream.count += 1
        return o

    def barrier(self):
        last = dict(self.cnt)
        for k in self.e:
            o = Op(k, lambda e: e.nop())
            o.epoch = self.epoch
            o.idx = self.cnt[k]
            self.cnt[k] += 1
            for k2, n in last.items():
                if n > 0:
                    o.ewait[k2] = n
            for s in self.streams:
                if s.count > 0:
                    o.swait[s] = s.count
            self.ops[k].append(o)

    def new_epoch(self):
        self.epoch += 1
        self.sems.append({k: self.nc.alloc_semaphore("sem%d_%s" % (self.epoch, k)) for k in self.e})
        self.cnt = {k: 0 for k in self.e}
        for b in Buf.ALL:
            b.w = None
            b.rs = {}

    def emit(self):
        nc = self.nc
        with nc.Block() as block:
            def run(k, eng):
                waited = {}
                ep = 0
                for o in self.ops[k]:
                    if o.epoch != ep:
                        ep = o.epoch
                        waited = {kk: vv for kk, vv in waited.items() if isinstance(kk, Stream)}
                    pend = []
                    for ek, v in o.ewait.items():
                        if waited.get(ek, 0) < v:
                            pend.append((self.sems[ep][ek], v))
                            waited[ek] = v
                    for s, c in o.swait.items():
                        if waited.get(s, 0) < c:
                            pend.append((s.sem, 16 * c))
                            waited[s] = c
                    fuse = None
                    if pend and FUSE_WAIT and k != "sp" and o.stream is None:
                        fuse = pend.pop()
                    for (sm, v) in pend:
                        eng.wait_ge(sm, v)
                    ins = o.fn(eng)
                    if fuse is not None:
                        ins._wait_ge(fuse[0], fuse[1])
                    if o.stream is not None:
                        ins.then_inc(o.stream.sem, 16)
                    else:
                        ins.then_inc(self.sems[ep][k], 1)

            @block.tensor
            def _(e):
                run("pe", e)

            @block.scalar
            def _(e):
                run("act", e)

            @block.vector
            def _(e):
                run("dve", e)

            @block.gpsimd
            def _(e):
                run("pool", e)

            @block.sync
            def _(e):
                run("sp", e)


def _rel_bucket(d):
    n = np.maximum(d, 0)
    me = 16
    nf = np.maximum(n, 1).astype(np.float32)
    large = me + (np.log(nf / np.float32(me)) / np.float32(math.log(2048 / me))
                  * np.float32(32 - me)).astype(np.int32)
    large = np.minimum(large, 31)
    return np.where(n < me, n, large)


CB_ID, CB_J, CB_ID3, CB_E8, CB_N = 0, 128, 256, 640, 1664
CF_SWAP, CF_CAUS, CF_POW, CF_CM, CF_Z, CF_N = 0, 128, 256, 288, 672, 704


def make_consts():
    cb = np.zeros((128, CB_N), np.float32)
    cb[:, CB_ID:CB_ID + 128] = np.eye(128)
    cb[:, CB_J:CB_J + 128] = np.eye(128)[::-1]
    for a in range(3):
        cb[:, CB_ID3 + 128 * a:CB_ID3 + 128 * (a + 1)] = np.eye(128)
    for n in range(8):
        cb[n, CB_E8 + 128 * n:CB_E8 + 128 * (n + 1)] = 1.0
    cf = np.zeros((128, CF_N), np.float32)
    sw = np.zeros((128, 128), np.float32)
    for i in range(128):
        sw[i, (i + 64) % 128] = 1.0
    cf[:, CF_SWAP:CF_SWAP + 128] = sw
    tl = np.arange(128)[:, None]
    sl = np.arange(128)[None, :]
    cf[:, CF_CAUS:CF_CAUS + 128] = np.where(sl > tl, -1e30, 0.0)
    for k in range(32):
        cf[:, CF_POW + k] = 2.0 ** (-(k + 1))
    for cur in range(8):
        m = np.zeros((6, 8), np.float32)
        m[:, cur:] = -1e30
        cf[:, CF_CM + 48 * cur:CF_CM + 48 * (cur + 1)] = m.reshape(-1)
    oh = np.zeros((33, FW), np.float32)
    for i in range(FW):
        d = i - 127
        if d < 0 or d > 2047:
            oh[32, i] = 1.0
        else:
            oh[int(_rel_bucket(np.array([d]))[0]), i] = 8.0
    ohb = np.zeros((33, 3 * FWB), np.float32)
    for g, dil in enumerate((1, 4, 16)):
        for i in range(FWB):
            dl = i - 127
            if dl < 0 or dl > 128:
                ohb[32, g * FWB + i] = 1.0
            else:
                ohb[int(_rel_bucket(np.array([dl * dil]))[0]), g * FWB + i] = 8.0
    return (cb.astype(ml_dtypes.bfloat16), cf, oh, ohb)


A_CONST = 0
A_XT = 10240
A_OT = A_XT + 32768
A_WR = A_OT + 32768
A_QK = A_WR + 16384
A_VV = A_QK + 36864
A_TAB = A_VV + 25088
A_WORK = A_TAB + 24576
A_END = A_WORK + 32768
ROW = A_END // 2


def build(NB, NL, dbg=(), stop=None):
    nc = bass.Bass("TRN2", target_bir_lowering=False)
    Buf.ALL = []
    P = Prog(nc)

    def din(name, shape, dt=F32):
        return nc.dram_tensor(name, list(shape), dt, kind="ExternalInput")

    x_h = din("x", [NB, S, D])
    p_h = din("p", [2, NB, S, 256])
    w_in_h = din("w_in", [2, D, 3528])
    w_gate_h = din("w_gate", [2, D, 3072])
    w_bra_h = din("w_br_a", [2, 384, D])
    w_brb_h = din("w_br_b", [2, 256, D])
    w_brc_h = din("w_br_c", [2, 384, D])
    w_out_h = din("w_out", [2, D, D])
    w_up_h = din("w_up", [2, D, 4096])
    w_down_h = din("w_down", [2, 4096, D])
    w_pg_h = din("w_ple_gate", [2, D, D])
    w_ple_h = din("w_ple", [2, 256, D])
    ln1g_h = din("ln1_g", [2, D])
    ln1b_h = din("ln1_b", [2, D])
    ln2g_h = din("ln2_g", [2, D])
    ln2b_h = din("ln2_b", [2, D])
    relb_h = din("rel_bias", [32, 24])
    cb_h = din("c_bf", [128, CB_N], BF16)
    cf_h = din("c_f32", [128, CF_N])
    oh_h = din("c_oh", [33, FW])
    ohb_h = din("c_ohb", [33, 3 * FWB])
    out_h = nc.dram_tensor("out", [NB, S, D], F32, kind="ExternalOutput")
    dbg_h = {}
    for name, shape, dt in dbg:
        dbg_h[name] = nc.dram_tensor("dbg_" + name, list(shape), dt, kind="ExternalOutput")

    wi_s = nc.dram_tensor("wi_s", [2, D, WIC], BF16)
    wg_s = nc.dram_tensor("wg_s", [2, D, 3072], BF16)
    wbr_s = nc.dram_tensor("wbr_s", [2, 1024, D], BF16)
    wo_s = nc.dram_tensor("wo_s", [2, D, D], BF16)
    wu_s = nc.dram_tensor("wu_s", [2, D, 4096], BF16)
    wd_s = nc.dram_tensor("wd_s", [2, 4096, D], BF16)
    wpg_s = nc.dram_tensor("wpg_s", [2, D, D], BF16)
    wpl_s = nc.dram_tensor("wpl_s", [2, 256, D], BF16)
    xres_s = nc.dram_tensor("xres_s", [S, D], F32)
    frow_s = nc.dram_tensor("frow_s", [24, FW], BF16)
    frowb_s = nc.dram_tensor("frowb_s", [3, 24, FWB], BF16)
    eg_s = nc.dram_tensor("eg_s", [2, 128, 6 * S], BF16)

    arena = nc.alloc_sbuf_tensor("arena", [128, ROW], BF16)

    def view(off, shape, dt=BF16):
        n = int(np.prod(shape))
        esz = 2 if dt == BF16 else 4
        a = arena[:, off // 2: off // 2 + n * esz // 2]
        if dt == F32:
            a = a.bitcast(F32)
        if len(shape) == 2:
            a = a.rearrange("p (a b) -> p a b", a=shape[0], b=shape[1])
        elif len(shape) == 3:
            a = a.rearrange("p (a b c) -> p a b c", a=shape[0], b=shape[1], c=shape[2])
        return a

    def raw(off_el, dims, p0=0, np_=128):
        return bass.AP(arena, p0 * ROW + off_el, [[ROW, np_]] + [list(d) for d in dims])

    PSH = [nc.alloc_psum_tensor("ps%d" % i, [128, 512], F32) for i in range(8)]
    PS = [h[:, :] for h in PSH]
    PSb = [Buf("ps%d" % i) for i in range(8)]

    cbv = view(A_CONST, [CB_N])
    cfv = view(A_CONST + 3328, [CF_N], F32)
    TBv = view(A_CONST + 6144, [24], F32)
    IWv = view(A_CONST + 6240, [NT, 8], F32)
    SMv = view(A_CONST + 6752, [64], F32)
    STv = view(A_CONST + 7008, [16], F32)
    KMf = view(A_CONST + 7072, [3, 8], F32)
    KMp = view(A_CONST + 7168, [3, 48])
    GSB = view(A_CONST + 7456, [48], F32)
    M8 = view(A_CONST + 7648, [6, 8], F32)
    NSEL = view(A_CONST + 7840, [6, 8])
    ONESf = view(A_CONST + 9728, [128], F32)
    CONSTb = Buf("const")
    ident = cbv[:, CB_ID:CB_ID + 128]
    Jm = cbv[:, CB_J:CB_J + 128]
    id3 = cbv[:, CB_ID3:CB_ID3 + 384]
    swapm = cfv[:, CF_SWAP:CF_SWAP + 128]
    causn = cfv[:, CF_CAUS:CF_CAUS + 128]
    zcol = cfv[:, CF_Z:CF_Z + 1]

    XT = view(A_XT, [8, S])
    XTb = [Buf("xt%d" % j) for j in range(NT)]
    OT = view(A_OT, [8, S])
    OTb = [Buf("ot%d" % c) for c in range(8)]
    QK = view(A_QK, [9, S])
    QKb = [Buf("qk%d" % c) for c in range(9)]
    VVb = Buf("vv")
    TABb = Buf("tab")
    VV_EL = A_VV // 2
    OA_EL = VV_EL + 64 + 12288

    st_c = P.stream()
    st_w = [P.stream() for _ in range(4)]
    st_x = [P.stream() for _ in range(2)]
    st_o = [P.stream() for _ in range(2)]
    st_t = P.stream()
    st_m = P.stream()

    psrr = [0]

    def nextps():
        i = psrr[0] % 8
        psrr[0] += 1
        return i

    evrr = [0]

    def evac(out, in_, r, w, engs=("act", "dve")):
        k = engs[evrr[0] % len(engs)]
        evrr[0] += 1
        if k == "act":
            P.op("act", lambda e: e.activation(out=out, in_=in_, func=AF.Copy), r, w)
        elif k == "dve":
            P.op("dve", lambda e: e.tensor_copy(out=out, in_=in_), r, w)
        else:
            P.op("pool", lambda e: e.tensor_copy(out=out, in_=in_), r, w)

    def mm(out, lhsT, rhs, start, stop, r, w):
        P.op("pe", lambda e: e.matmul(out, lhsT, rhs, start=start, stop=stop), r, w)

    def dump(name, src, r):
        if name in dbg_h:
            P.dma(st_m, dbg_h[name].ap(), src, r=r)

    P.dma(st_c, cbv, cb_h.ap(), w=[CONSTb])
    P.dma(st_c, cfv, cf_h.ap(), w=[CONSTb], new_batch=False)
    P.dma(st_c, TBv[0:32, :], relb_h.ap(), w=[CONSTb], new_batch=False)
    P.op("dve", lambda e: e.memset(TBv[32:33, :], -BIG), w=[CONSTb])
    P.op("dve", lambda e: e.memset(ONESf[0:1, :], 1.0), w=[CONSTb])

    ohv = view(A_WORK, [FW], F32)
    ohbv = view(A_WORK + 4 * FW, [3 * FWB], F32)
    frs = view(A_WORK + 4 * FW + 12 * FWB, [FW])
    frbs = view(A_WORK + 6 * FW + 12 * FWB, [3 * FWB])
    Wb = Buf("work")
    P.dma(st_m, ohv[0:33, :], oh_h.ap(), w=[Wb])
    P.dma(st_m, ohbv[0:33, :], ohb_h.ap(), w=[Wb], new_batch=False)
    for c0 in range(0, FW, 512):
        wd_ = min(512, FW - c0)
        i = nextps()
        mm(PS[i][0:24, :wd_], TBv[0:33, :], ohv[0:33, c0:c0 + wd_], True, True, [CONSTb, Wb], [PSb[i]])
        evac(frs[0:24, c0:c0 + wd_], PS[i][0:24, :wd_], [PSb[i]], [Wb], engs=("dve",))
    for c0 in range(0, 3 * FWB, 512):
        i = nextps()
        mm(PS[i][0:24, :512], TBv[0:33, :], ohbv[0:33, c0:c0 + 512], True, True, [CONSTb, Wb], [PSb[i]])
        evac(frbs[0:24, c0:c0 + 512], PS[i][0:24, :512], [PSb[i]], [Wb], engs=("dve",))
    FRb = Buf("frow")
    P.dma(st_m, frow_s.ap(), frs[0:24, :], r=[Wb], w=[FRb])
    for g in range(3):
        P.dma(st_m, frowb_s.ap()[g], frbs[0:24, g * FWB:(g + 1) * FWB], r=[Wb], w=[FRb],
              new_batch=False)
    P.barrier()
    Gset = view(A_TAB, [6 * S])
    EGst = view(A_QK, [6 * S])
    EGb = Buf("egst")
    for tbl, h0 in ((1, 18),):
        P.dma(st_t, Gset.rearrange("p (a b) -> p a b", a=6), bass.AP(frow_s, h0 * FW, [[1, 128], [FW, 6], [1, S]]),
              r=[FRb], w=[TABb])
        for ci in range(24):
            def one(ci=ci):
                i = nextps()
                mm(PS[i], Jm, Gset[:, ci * 512:(ci + 1) * 512], True, True, [CONSTb, TABb], [PSb[i]])
                P.op("act", lambda e: e.activation(out=EGst[:, ci * 512:(ci + 1) * 512], in_=PS[i], func=AF.Exp, scale=0.125),
                     [PSb[i]], [EGb])
            one()
        P.dma(st_m, eg_s.ap()[tbl], EGst, r=[EGb], w=[FRb])
    P.barrier()
    if stop == "setup":
        P.emit()
        return nc

    WSb = Buf("wscratch")
    NSLOT = 4
    stg = [view(A_XT + i * 16384, [4096], F32) for i in range(NSLOT)]
    stgb = [Buf("stg%d" % i) for i in range(NSLOT)]
    cst = [view(A_QK + i * 8192, [4096]) for i in range(NSLOT)]
    cstb = [Buf("cst%d" % i) for i in range(NSLOT)]
    st_pl = [P.stream() for _ in range(NSLOT)]
    st_ps = [P.stream() for _ in range(NSLOT)]
    pc = [0]
    WI_PIECES = [(0, 448, 0), (384, 448, 448), (512, 1088, 512), (1024, 1088, 1088),
                 (1096, 2120, 1152), (2376, 3144, 2176), (448, 512, 2944), (1088, 1096, 3008),
                 (2120, 2376, 3016), (3144, 3528, 3272)]

    def precast(src, dst, C, C2, pieces=None):
        R = src.shape[0]
        for rc in range(R // 128):
            k = pc[0] % NSLOT
            pc[0] += 1
            P.dma(st_pl[k], stg[k][:, :C], src[rc * 128:(rc + 1) * 128, :], w=[stgb[k]])
            eng = ("dve", "pool", "dve")[pc[0] % 3]
            for (s0, s1, d0) in (pieces or [(0, C, 0)]):
                o_, i_ = cst[k][:, d0:d0 + s1 - s0], stg[k][:, s0:s1]
                P.op(eng, lambda e, o_=o_, i_=i_: e.tensor_copy(out=o_, in_=i_), [stgb[k]], [cstb[k]])
            P.dma(st_ps[k], dst[rc * 128:(rc + 1) * 128, :], cst[k][:, :C2], r=[cstb[k]], w=[WSb], eng="act")

    for L in range(NL):
        precast(w_in_h.ap()[L], wi_s.ap()[L], 3528, WIC, WI_PIECES)
        for hh in range(2):
            precast(w_gate_h.ap()[L][:, hh * 1536:(hh + 1) * 1536], wg_s.ap()[L][:, hh * 1536:(hh + 1) * 1536], 1536, 1536)
        precast(w_bra_h.ap()[L], wbr_s.ap()[L][0:384], D, D)
        precast(w_brb_h.ap()[L], wbr_s.ap()[L][384:640], D, D)
        precast(w_brc_h.ap()[L], wbr_s.ap()[L][640:1024], D, D)
        precast(w_out_h.ap()[L], wo_s.ap()[L], D, D)
        for hh in range(2):
            precast(w_up_h.ap()[L][:, hh * 2048:(hh + 1) * 2048], wu_s.ap()[L][:, hh * 2048:(hh + 1) * 2048], 2048, 2048)
        precast(w_down_h.ap()[L], wd_s.ap()[L], D, D)
        precast(w_pg_h.ap()[L], wpg_s.ap()[L], D, D)
        precast(w_ple_h.ap()[L], wpl_s.ap()[L], D, D)
    P.barrier()
    if stop == "precast":
        P.emit()
        return nc

    WR = [view(A_WR + i * 8192, [8, 512]) for i in range(2)]
    WRb = [Buf("wr%d" % i) for i in range(2)]
    wrr = [0]

    def wtile(src2d, c0, ncols):
        k = wrr[0] % 2
        wrr[0] += 1
        srcv = src2d.rearrange("(k p) c -> p k c", p=128)[:, :, c0:c0 + ncols]
        P.dma(st_w[k], WR[k][:, :, :ncols], srcv, r=[WSb], w=[WRb[k]])
        return WR[k], WRb[k]

    def proj_fm(wsrc, col0, nchunks, qk0):
        for g0 in range(0, nchunks, 4):
            n = min(4, nchunks - g0)
            wt, wb = wtile(wsrc, col0 + g0 * 128, n * 128)
            for ci in range(n):
                for tb in range(4):
                    i = nextps()
                    for kc in range(8):
                        mm(PS[i], wt[:, kc, ci * 128:(ci + 1) * 128], XT[:, kc, tb * 512:(tb + 1) * 512],
                           kc == 0, kc == 7, [wb] + XTb[4 * tb:4 * tb + 4], [PSb[i]])
                    evac(QK[:, qk0 + g0 + ci, tb * 512:(tb + 1) * 512], PS[i], [PSb[i]], [QKb[qk0 + g0 + ci]])

    def proj_tm(wsrc, col0, ncols, tok_fn, ntiles, dst_fn):
        wt, wb = wtile(wsrc, col0, ncols)
        for j in range(ntiles):
            i = nextps()
            st, step = tok_fn(j)
            for kc in range(8):
                mm(PS[i][:, :ncols], XT[:, kc, st:st + 127 * step + 1:step], wt[:, kc, :ncols],
                   kc == 0, kc == 7, [wb] + XTb, [PSb[i]])
            dst_fn(j, PS[i][:, :ncols], PSb[i])

    def vlhs(hf, vcol):
        if hf == 0:
            return raw(vcol, [[1, 128]])
        return raw(vcol - 64, [[1, 128]])

    def load_x_tiles(src_fn, L):
        xs = [view(A_WORK + i * 4096, [D], F32) for i in range(2)]
        xsb = [Buf("xs%d" % i) for i in range(2)]
        xb = [view(A_WORK + 8192 + i * 2048, [D]) for i in range(2)]
        xbb = [Buf("xb%d" % i) for i in range(2)]
        for j in range(NT):
            k = j % 2
            P.dma(st_x[k], xs[k], src_fn(j), w=[xsb[k]])
            evac(xb[k], xs[k], [xsb[k]], [xbb[k]], engs=("act",))
            to_xt(j, xb[k], xbb[k])

    def to_xt(j, xb_ap, xb_buf):
        for half in range(2):
            i = nextps()
            pbf = PS[i].bitcast(BF16)
            for q in range(4):
                kc = half * 4 + q
                P.op("pe", lambda e, o_=pbf[:, q * 128:(q + 1) * 128], i_=xb_ap[:, kc * 128:(kc + 1) * 128]:
                     e.transpose(o_, i_, ident), [xb_buf, CONSTb], [PSb[i]])
            evac(XT[:, half * 4:half * 4 + 4, j * 128:(j + 1) * 128],
                 pbf[:, 0:512].rearrange("p (a b) -> p a b", a=4), [PSb[i]], [XTb[j]])

    def finish_heads(psO_i, rows_list, dst_list, ncols, osb, osbb):
        P.op("act", lambda e: e.activation(out=osb[:, :ncols], in_=PS[psO_i][:, :ncols], func=AF.Copy),
             [PSb[psO_i]], [osbb])
        i = nextps()
        mm(PS[i][:, :ncols], swapm, osb[:, :ncols], True, True, [CONSTb, osbb], [PSb[i]])
        for (rows, c0, c1, dst, dbuf) in zip(*[rows_list] * 1, *[[]] * 0) if False else []:
            pass
        for (rows, c0, c1, dst, dbuf) in dst_list:
            P.op("dve", lambda e, rows=rows, c0=c0, c1=c1, dst=dst:
                 e.tensor_tensor(out=dst, in0=osb[rows, c0:c1], in1=PS[i][rows, c0:c1], op=ALU.divide),
                 [osbb, PSb[i]], [dbuf])

    def load_tab(src_ap, dst_view):
        P.dma(st_t, dst_view, src_ap, r=[FRb], w=[TABb])

    def attn_finish(oi, ncols, osb, osbb, rd, rdb, hf, dsts, fin_bank=7):
        rows = slice(64 * hf, 64 * hf + 64)
        P.op("act", lambda e: e.activation(out=osb[:, :ncols], in_=PS[oi][:, :ncols], func=AF.Copy), [PSb[oi]], [osbb])
        fb = fin_bank
        mm(PS[fb][:, :ncols], swapm, osb[:, :ncols], True, True, [CONSTb, osbb], [PSb[fb]])
        P.op("dve", lambda e: e.reciprocal(out=rd[rows, :ncols], in_=PS[fb][rows, :ncols]), [PSb[fb]], [rdb])
        P.op("dve", lambda e: e.tensor_tensor(out=rd[rows, :ncols], in0=osb[rows, :ncols], in1=rd[rows, :ncols],
                                              op=ALU.mult), [osbb, rdb], [rdb])
        for (c0, dst, dbuf) in dsts:
            P.op("act", lambda e, c0=c0, dst=dst: e.activation(out=dst, in_=rd[rows, c0:c0 + 128], func=AF.Copy),
                 [rdb], [dbuf])

    def mmg(out, lhsT, rhs, start, stop, r, w):
        P.op("pe", lambda e: e.matmul(out, lhsT, rhs, start=start, stop=stop, skip_group_check=True), r, w)

    def run_pipe(steps, look=2):
        n = len(steps)
        if n == 0:
            return

        def doL(i):
            if steps[i][0]:
                steps[i][0]()
            steps[i][1]()
        for i in range(min(look, n)):
            doL(i)
        for i in range(n):
            if i + look < n:
                doL(i + look)
            steps[i][2]()
            steps[i][3]()

    def phase_A(wi):
        proj_fm(wi, 0, 9, 0)
        P.op("dve", lambda e: e.memset(raw(VV_EL, [[192, 16], [1, 64]]), 1.0), w=[VVb])
        P.op("dve", lambda e: e.memset(raw(VV_EL + 128, [[192, 16], [1, 64]]), 1.0), w=[VVb])

        def a_dst(j, ps, psb):
            vc = raw(VV_EL + 64 + 192 * j, [[1, 64]])
            P.op("act", lambda e: e.activation(out=vc, in_=ps[:, 0:64], func=AF.Copy), [psb], [VVb])
            P.op("dve", lambda e: e.tensor_copy(out=IWv[:, j, :], in_=ps[:, 64:72]), [psb], [CONSTb])
        proj_tm(wi, 2944, 72, lambda j: (128 * j, 1), NT, a_dst)
        P.barrier()
        G = view(A_TAB, [6, S])
        load_tab(bass.AP(frow_s, 0, [[1, 128], [FW, 6], [1, S]]), G)
        Ibuf = view(A_WORK, [S], F32)
        Ib = Buf("I")
        NM = [view(A_WORK + 8192 + i * 4096, [S]) for i in range(2)]
        NMb = [Buf("nm%d" % i) for i in range(2)]
        Dg = [view(A_WORK + 16384 + i * 2048, [8, 128]) for i in range(2)]
        Dgb = [Buf("dg%d" % i) for i in range(2)]
        Rr = [view(A_WORK + 20480 + i * 1024, [512]) for i in range(3)]
        Rb = [Buf("r%d" % i) for i in range(3)]
        PT = [view(A_WORK + 23552 + i * 768, [384]) for i in range(4)]
        PTb = [Buf("pt%d" % i) for i in range(4)]
        Osb = [view(A_WORK + 26624 + i * 1536, [384], F32) for i in range(2)]
        Osbb = [Buf("osb%d" % i) for i in range(2)]
        Rd = [view(A_WORK + 29696 + i * 1536, [384], F32) for i in range(2)]
        Rdb = [Buf("rd%d" % i) for i in range(2)]
        SMb = Buf("sm")
        rk = [0]
        pk = [0]

        Ibufs = [Ibuf, view(A_VV + 8192, [S], F32)]
        Ibs = [Ib, Buf("I2")]
        NITA = 14
        lring = [3, 4, 7]
        lk = [0]

        NM4 = [NM[0], NM[1], view(A_VV + 16384, [S]), view(A_VV + 20480, [S])]
        NM4b = [NMb[0], NMb[1], Buf("nm2"), Buf("nm3")]
        SMbs = [Buf("sm0"), Buf("sm1")]

        def idx_only(j):
            t0 = 128 * j
            N = t0 + 128
            Iv, Ivb = Ibufs[j % 2], Ibs[j % 2]
            dg, dgb = Dg[j % 2], Dgb[j % 2]
            for h in range(8):
                P.op("pool", lambda e, h=h: e.tensor_scalar(out=dg[:, h, :], in0=ident, scalar1=IWv[:, j, h:h + 1],
                                                           scalar2=None, op0=ALU.mult), [CONSTb], [dgb])
            for sb in range((N + 511) // 512):
                c0 = 512 * sb
                ws = min(512, N - c0)

                def S_(h, c0=c0, ws=ws):
                    rows = slice(64 * (h % 2), 64 * (h % 2) + 64)
                    si = h % 2
                    mm(PS[si][:, :ws], QK[rows, 4 + h // 2, t0:t0 + 128], QK[rows, 8, c0:c0 + ws], True, True,
                       [QKb[4 + h // 2], QKb[8]], [PSb[si]])

                def RD_(h, c0=c0, ws=ws):
                    si = h % 2
                    rr, rb = Rr[rk[0] % 3], Rb[rk[0] % 3]
                    rk[0] += 1
                    P.op("act", lambda e: e.activation(out=rr[:, :ws], in_=PS[si][:, :ws], func=AF.Relu), [PSb[si]], [rb])
                    mm(PS[2][:, :ws], dg[:, h, :], rr[:, :ws], h == 0, h == 7, [dgb, rb], [PSb[2]])
                S_(0)
                for h in range(8):
                    if h + 1 < 8:
                        S_(h + 1)
                    RD_(h)
                P.op("dve", lambda e, c0=c0, ws=ws: e.tensor_copy(out=Iv[:, c0:c0 + ws], in_=PS[2][:, :ws]), [PSb[2]], [Ivb])
            P.op("dve", lambda e: e.tensor_tensor(out=Iv[:, t0:N], in0=Iv[:, t0:N], in1=causn, op=ALU.add), [Ivb, CONSTb], [Ivb])

        def bis_ops(j):
            if j < 2:
                return []
            t0 = 128 * j
            N = t0 + 128
            Iv, Ivb = Ibufs[j % 2], Ibs[j % 2]
            nm, nmb = NM4[j % 4], NM4b[j % 4]
            smb = SMbs[j % 2]
            o0 = 32 * (j % 2)
            hi, lo, wd0, mid, cnt, tt_ = [SMv[:, o0 + q:o0 + q + 1] for q in range(6)]
            WD = SMv[:, o0 + 8:o0 + 8 + NITA + 1]
            ops = []
            ops.append(lambda: P.op("dve", lambda e: e.tensor_reduce(out=hi, in_=Iv[:, :N], axis=AX.X, op=ALU.max), [Ivb], [smb]))
            ops.append(lambda: P.op("dve", lambda e: e.tensor_reduce(out=lo, in_=Iv[:, :t0], axis=AX.X, op=ALU.min), [Ivb], [smb]))
            ops.append(lambda: P.op("dve", lambda e: e.tensor_tensor(out=wd0, in0=hi, in1=lo, op=ALU.subtract), [smb], [smb]))
            ops.append(lambda: P.op("dve", lambda e: e.tensor_scalar(out=WD, in0=cfv[:, CF_POW:CF_POW + NITA + 1], scalar1=wd0,
                                                                      scalar2=None, op0=ALU.mult), [smb, CONSTb], [smb]))
            ops.append(lambda: P.op("dve", lambda e: e.tensor_tensor(out=mid, in0=lo, in1=WD[:, 0:1], op=ALU.add), [smb], [smb]))
            for k in range(NITA):
                ops.append(lambda: P.op("dve", lambda e: e.scalar_tensor_tensor(
                    out=nm[:, :N], in0=Iv[:, :N], scalar=mid, in1=zcol.to_broadcast([128, N]), op0=ALU.is_ge, op1=ALU.add,
                    accum_out=cnt), [Ivb, smb, CONSTb], [nmb, smb]))
                ops.append(lambda: P.op("dve", lambda e: e.tensor_scalar(out=tt_, in0=cnt, scalar1=256.0, scalar2=0.5, op0=ALU.is_ge,
                                                                          op1=ALU.subtract), [smb], [smb]))
                ops.append(lambda k=k: P.op("dve", lambda e: e.scalar_tensor_tensor(out=mid, in0=tt_, scalar=WD[:, k:k + 1], in1=mid,
                                                                                    op0=ALU.mult, op1=ALU.add), [smb], [smb]))
            ops.append(lambda: P.op("dve", lambda e: e.tensor_tensor(out=lo, in0=mid, in1=WD[:, NITA:NITA + 1], op=ALU.subtract),
                                    [smb], [smb]))
            ops.append(lambda: P.op("dve", lambda e: e.tensor_scalar(out=nm[:, :N], in0=Iv[:, :N], scalar1=lo, scalar2=-BIG,
                                                                      op0=ALU.is_lt, op1=ALU.mult), [Ivb, smb], [nmb]))
            return ops

        def idx_bis_pair(j0):
            idx_only(j0)
            idx_only(j0 + 1)
            oa, ob = bis_ops(j0), bis_ops(j0 + 1)
            for i in range(max(len(oa), len(ob))):
                if i < len(oa):
                    oa[i]()
                if i < len(ob):
                    ob[i]()

        def att_steps(j):
            t0 = 128 * j
            nm, nmb = NM4[j % 4], NM4b[j % 4]
            steps = []
            for c in range(j + 1):
                for hf in range(2):
                    def mk(c=c, hf=hf):
                        s0 = 128 * c
                        w = t0 - s0
                        rows = slice(64 * hf, 64 * hf + 64)
                        st = {}

                        def L():
                            li = lring[lk[0] % 3]
                            lk[0] += 1
                            st["li"] = li
                            pl3 = PS[li][:, :384].rearrange("p (a b) -> p a b", a=3)
                            mm(pl3, QK[rows, 3, s0:s0 + 128], QK[rows, 0:3, t0:t0 + 128], True, False,
                               [QKb[0], QKb[1], QKb[2], QKb[3]], [PSb[li]])
                            mm(pl3, Jm, G[:, hf:6:2, w:w + 128], False, j < 2, [CONSTb, TABb], [PSb[li]])
                            if j >= 2:
                                mm(PS[li][:, :384], nm[:, s0:s0 + 128], id3, False, True, [nmb, CONSTb], [PSb[li]])

                        def E():
                            li = st["li"]
                            pt, ptb = PT[pk[0] % 4], PTb[pk[0] % 4]
                            pk[0] += 1
                            st["pt"] = (pt, ptb)
                            P.op("act", lambda e: e.activation(out=pt, in_=PS[li][:, :384], func=AF.Exp, scale=0.125),
                                 [PSb[li]], [ptb])

                        def PV():
                            pt, ptb = st["pt"]
                            oi = 5 + hf
                            mm(PS[oi][:, :384], vlhs(hf, VV_EL + 64 + 192 * c), pt, c == 0, c == j, [VVb, ptb], [PSb[oi]])
                            if c == j:
                                attn_finish(oi, 384, Osb[hf], Osbb[hf], Rd[hf], Rdb[hf], hf,
                                            [(128 * cc, OT[rows, cc, t0:t0 + 128], OTb[cc]) for cc in range(3)], fin_bank=2)
                        return [None, L, E, PV]
                    steps.append(mk())
            return steps

        allsteps = []
        idx_bis_pair(0)
        for j in range(NT):
            st_j = att_steps(j)
            if j % 2 == 0 and j + 2 < NT:
                st_j[0][0] = (lambda jj=j + 2: idx_bis_pair(jj))
            allsteps += st_j
        run_pipe(allsteps)
        P.barrier()

    def phase_C(wi):
        proj_fm(wi, 2176, 6, 0)
        P.op("dve", lambda e: e.memset(raw(VV_EL + 64, [[192, 48], [1, 64]]), 1.0), w=[VVb])

        def c_dst(j, ps, psb):
            for a in range(3):
                o_ = raw(VV_EL + 576 * j + 192 * a, [[128, 2], [1, 64]])
                i_ = ps[:, 128 * a:128 * a + 128].rearrange("p (b c) -> p b c", b=2)
                P.op("act", lambda e, o_=o_, i_=i_: e.activation(out=o_, in_=i_, func=AF.Copy), [psb], [VVb])
        proj_tm(wi, 3272, 384, lambda j: (128 * j, 1), NT, c_dst)
        P.barrier()
        if stop == "C1":
            return
        G = view(A_TAB, [6, S])
        load_tab(eg_s.ap()[1].rearrange("p (a b) -> p a b", a=6), G)
        NMT = view(A_WORK, [6, S])
        NMTb = Buf("nmt")
        PT = [view(A_WORK + 24576 + i * 768, [384]) for i in range(3)]
        PTb = [Buf("pt%d" % i) for i in range(3)]
        Osb, Osbb = view(A_WORK + 26880, [384], F32), Buf("osb")
        Rd, Rdb = view(A_WORK + 28416, [384], F32), Buf("rd")
        GPb = Buf("gp")
        MX = view(A_WORK + 29952, [6], F32)
        EQ = view(A_WORK + 30016, [6, 8], F32)
        G2 = view(A_WORK + 30208, [6, 8], F32)
        G3 = view(A_WORK + 30400, [6, 8], F32)
        ck4 = QK[:, 3:6, :].rearrange("p c (n k) -> p c n k", k=256)
        P.op("dve", lambda e: e.tensor_reduce(out=KMf, in_=ck4, axis=AX.X, op=ALU.add), [QKb[3], QKb[4], QKb[5]], [GPb])
        P.op("dve", lambda e: e.memset(KMp, 0.0), [], [GPb])
        for c in range(3):
            for e_ in range(2):
                h = 2 * c + e_
                rows = slice(64 * e_, 64 * e_ + 64)
                P.op("dve", lambda e, rows=rows, c=c, h=h: e.tensor_copy(out=KMp[rows, c, 8 * h:8 * h + 8], in_=KMf[rows, c, :]),
                     [GPb], [GPb])
        E8v = cbv[0:8, CB_E8:CB_E8 + 1024]
        pk = [0]
        if stop == "C2":
            P.barrier()
            return

        def gate_tile(j):
            cur = j // 2
            t0 = 128 * j
            gis = [nextps(), nextps()]
            for e_ in range(2):
                rows = slice(64 * e_, 64 * e_ + 64)
                for c in range(3):
                    mm(PS[gis[e_]][:, :48], QK[rows, c, t0:t0 + 128], KMp[rows, c, :], c == 0, c == 2, [QKb[c], GPb], [PSb[gis[e_]]])
            P.op("dve", lambda e: e.tensor_tensor(out=GSB, in0=PS[gis[0]][:, :48], in1=cfv[:, CF_CM + 48 * cur:CF_CM + 48 * cur + 48],
                                                  op=ALU.add), [PSb[gis[0]], CONSTb], [GPb])
            P.op("dve", lambda e: e.tensor_tensor(out=GSB, in0=GSB, in1=PS[gis[1]][:, :48], op=ALU.add), [PSb[gis[1]], GPb], [GPb])
            if stop == "C3a":
                return
            g3v = GSB.rearrange("p (a b) -> p a b", a=6)
            cur_src = g3v
            for it in range(2):
                P.op("dve", lambda e, cur_src=cur_src: e.tensor_reduce(out=MX, in_=cur_src, axis=AX.X, op=ALU.max), [GPb], [GPb])
                P.op("dve", lambda e, cur_src=cur_src: e.tensor_tensor(out=EQ, in0=cur_src, in1=MX.unsqueeze(2).to_broadcast([128, 6, 8]),
                                                                        op=ALU.is_ge), [GPb], [GPb])
                dst = G2 if it == 0 else G3
                P.op("dve", lambda e, cur_src=cur_src, dst=dst: e.scalar_tensor_tensor(out=dst, in0=EQ, scalar=-3e30, in1=cur_src,
                                                                                       op0=ALU.mult, op1=ALU.add), [GPb], [GPb])
                cur_src = dst
            P.op("dve", lambda e: e.tensor_reduce(out=MX, in_=G3, axis=AX.X, op=ALU.max), [GPb], [GPb])
            P.op("dve", lambda e: e.tensor_tensor(out=EQ, in0=g3v, in1=MX.unsqueeze(2).to_broadcast([128, 6, 8]), op=ALU.is_lt),
                 [GPb], [GPb])
            P.op("dve", lambda e: e.tensor_scalar(out=NSEL, in0=EQ, scalar1=-BIG, scalar2=None, op0=ALU.mult), [GPb], [GPb])
            if stop == "C3b":
                return
            for h3 in range(2):
                ti = nextps()
                for hh in range(3):
                    h = 3 * h3 + hh
                    mm(PS[ti][0:8, hh * 128:(hh + 1) * 128], NSEL[:, h, :], ident, True, True, [GPb, CONSTb], [PSb[ti]])
                P.op("act", lambda e, ti=ti, h3=h3: e.activation(
                    out=NMT[0:8, 3 * h3:3 * h3 + 3, t0:t0 + 128],
                    in_=PS[ti][0:8, :384].rearrange("p (a b) -> p a b", a=3), func=AF.Copy), [PSb[ti]], [NMTb])

        lk = [0]

        def c_steps(j):
            t0 = 128 * j
            steps = []
            for c in range(j + 1):
                for hf in range(2):
                    def mk(c=c, hf=hf):
                        s0 = 128 * c
                        w = t0 - s0
                        n = c // 2
                        rows = slice(64 * hf, 64 * hf + 64)
                        st = {}

                        def L():
                            li = lk[0] % 4
                            lk[0] += 1
                            st["li"] = li
                            pl3 = PS[li][:, :384].rearrange("p (a b) -> p a b", a=3)
                            own = (n == j // 2)
                            for cc in range(3):
                                mmg(PS[li][:, cc * 128:(cc + 1) * 128], QK[rows, 3 + cc, s0:s0 + 128], QK[rows, cc, t0:t0 + 128],
                                    cc == 0, own and cc == 2, [QKb[cc], QKb[3 + cc]], [PSb[li]])
                            if not own:
                                mmg(pl3, E8v[:, n * 128:(n + 1) * 128], NMT[0:8, hf:6:2, t0:t0 + 128], False, True,
                                    [CONSTb, NMTb], [PSb[li]])

                        def E():
                            li = st["li"]
                            pt, ptb = PT[pk[0] % 3], PTb[pk[0] % 3]
                            pk[0] += 1
                            st["pt"] = (pt, ptb)
                            P.op("act", lambda e: e.activation(out=pt, in_=PS[li][:, :384], func=AF.Exp, scale=0.125),
                                 [PSb[li]], [ptb])
                            pt3 = pt.rearrange("p (a b) -> p a b", a=3)
                            P.op("dve", lambda e: e.tensor_tensor(out=pt3, in0=pt3, in1=G[:, hf:6:2, w:w + 128], op=ALU.mult),
                                 [ptb, TABb], [ptb])

                        def PV():
                            pt, ptb = st["pt"]
                            oi = 4 + hf
                            for cc in range(3):
                                vcol = VV_EL + 576 * c + 192 * cc + (64 if hf else 0)
                                mmg(PS[oi][:, cc * 128:(cc + 1) * 128], raw(vcol, [[1, 128]]), pt[:, cc * 128:(cc + 1) * 128],
                                    c == 0 and cc == 0, c == j, [VVb, ptb], [PSb[oi]])
                            if c == j:
                                attn_finish(oi, 384, Osb, Osbb, Rd, Rdb, hf,
                                            [(128 * cc, OT[rows, 5 + cc, t0:t0 + 128], OTb[5 + cc]) for cc in range(3)], fin_bank=6)
                        return [None, L, E, PV]
                    steps.append(mk())
            return steps
        for j in range(NT):
            gate_tile(j)
        if stop in ("C3", "C3a", "C3b"):
            P.barrier()
            return
        allsteps = []
        for j in range(NT):
            allsteps += c_steps(j)
        run_pipe(allsteps)
        P.barrier()

    def phase_B(wi):
        proj_fm(wi, 1152, 8, 0)
        VB = A_VV // 2
        P.op("dve", lambda e: e.memset(raw(VB + 64, [[192, 96], [1, 64]]), 1.0), w=[VVb])
        for g, dil in enumerate((1, 4, 16)):
            def tokf(tile, g=g, dil=dil):
                if g == 0:
                    return (128 * tile, 1)
                if g == 1:
                    return ((tile // 4) + 4 * 128 * (tile % 4), 4)
                return (tile, 16)

            def b_dst(tile, ps, psb, g=g):
                for a in range(2):
                    o_ = raw(VB + (g * 16 + tile) * 384 + 192 * a, [[128, 2], [1, 64]])
                    i_ = ps[:, 128 * a:128 * a + 128].rearrange("p (b c) -> p b c", b=2)
                    P.op("act", lambda e, o_=o_, i_=i_: e.activation(out=o_, in_=i_, func=AF.Copy), [psb], [VVb])
            proj_tm(wi, 3016, 256, tokf, 16, b_dst)
        P.barrier()
        if stop == "B1":
            return
        GB = view(A_VV + 36864, [12, 256])
        for g in range(3):
            P.dma(st_t, GB[:, 4 * g:4 * g + 4, :],
                  bass.AP(frowb_s, g * 24 * FWB + (6 + 4 * g) * FWB, [[1, 128], [FWB, 4], [1, 256]]),
                  r=[FRb], w=[TABb], new_batch=(g == 0))
        ACC = view(A_WORK, [2, S], F32)
        ACCb = Buf("acc")
        PT = [view(A_WORK + 16384 + i * 512, [256]) for i in range(4)]
        PTb = [Buf("pt%d" % i) for i in range(4)]
        Rd, Rdb = view(A_WORK + 18432, [512], F32), Buf("rd")
        pk = [0]

        lk = [0]

        def group_steps(sp, g, dil, r, q):
            st_q = r + dil * 128 * q
            tq = slice(st_q, st_q + dil * 127 + 1, dil)
            scs = [q - 1, q] if q >= 1 else [q]
            steps = []
            for e_ in range(2):
                for si_, sc in enumerate(scs):
                    def mk(e_=e_, si_=si_, sc=sc):
                        rows = slice(64 * e_, 64 * e_ + 64)
                        oi = 4 + e_
                        st_s = r + dil * 128 * sc
                        tsl = slice(st_s, st_s + dil * 127 + 1, dil)
                        w = 128 * (q - sc)
                        st = {}

                        def L():
                            li = lk[0] % 4
                            lk[0] += 1
                            st["li"] = li
                            mm(PS[li][:, :128], QK[rows, 6 + sp, tsl], QK[rows, 2 * g + sp, tq], True, False,
                               [QKb[6 + sp], QKb[2 * g + sp]], [PSb[li]])
                            mm(PS[li][:, :128], Jm, GB[:, 4 * g + 2 * sp + e_, w:w + 128], False, True, [CONSTb, TABb], [PSb[li]])

                        def E():
                            li = st["li"]
                            pt, ptb = PT[pk[0] % 4], PTb[pk[0] % 4]
                            pk[0] += 1
                            st["pt"] = (pt, ptb)
                            P.op("act", lambda e: e.activation(out=pt[:, :128], in_=PS[li][:, :128], func=AF.Exp, scale=0.125),
                                 [PSb[li]], [ptb])

                        def PV():
                            pt, ptb = st["pt"]
                            tile = sc if g == 0 else (r * 4 + sc if g == 1 else r)
                            vcol = VB + (g * 16 + tile) * 384 + 192 * sp + (64 if e_ else 0)
                            mm(PS[oi][:, :128], raw(vcol, [[1, 128]]), pt[:, :128], si_ == 0, si_ == len(scs) - 1,
                               [VVb, ptb], [PSb[oi]])
                            if si_ == len(scs) - 1:
                                accv = ACC[:, e_, tq]
                                if g == 0:
                                    P.op("dve", lambda e: e.tensor_copy(out=accv, in_=PS[oi][:, :128]), [PSb[oi]], [ACCb])
                                else:
                                    P.op("dve", lambda e: e.tensor_tensor(out=accv, in0=accv, in1=PS[oi][:, :128], op=ALU.add),
                                         [PSb[oi], ACCb], [ACCb])
                        return [None, L, E, PV]
                    steps.append(mk())
            return steps

        for sp in range(2):
            allsteps = []
            for g, dil in enumerate((1, 4, 16)):
                for r in range(dil):
                    for q in range(16 // dil):
                        allsteps += group_steps(sp, g, dil, r, q)
            run_pipe(allsteps)
            for e_ in range(2):
                rows = slice(64 * e_, 64 * e_ + 64)
                for tb in range(4):
                    def fin(e_=e_, rows=rows, tb=tb, sp=sp):
                        i = nextps()
                        mm(PS[i], swapm, ACC[:, e_, tb * 512:(tb + 1) * 512], True, True, [CONSTb, ACCb], [PSb[i]])
                        P.op("dve", lambda e: e.reciprocal(out=Rd[rows, :], in_=PS[i][rows, :]), [PSb[i]], [Rdb])
                        P.op("dve", lambda e: e.tensor_tensor(out=OT[rows, 3 + sp, tb * 512:(tb + 1) * 512],
                                                              in0=ACC[rows, e_, tb * 512:(tb + 1) * 512], in1=Rd[rows, :],
                                                              op=ALU.mult), [ACCb, Rdb], [OTb[3 + sp]])
                    fin()
        P.barrier()

    def load_ln(dst, src_h, L, rowbuf, rowb, LNb):
        P.dma(st_m, rowbuf[0:1, :], src_h.ap()[L:L + 1, :], w=[rowb])
        for hh in range(2):
            i = nextps()
            mm(PS[i], ONESf[0:1, :], rowbuf[0:1, hh * 512:(hh + 1) * 512], True, True, [CONSTb, rowb], [PSb[i]])
            P.op("dve", lambda e, i=i, hh=hh: e.tensor_copy(out=dst[:, hh * 512:(hh + 1) * 512], in_=PS[i]), [PSb[i]], [LNb])

    STbs = [Buf("st0"), Buf("st1")]

    def ln_ops(xap, LG, LB, PARb, junk, xbuf, sset):
        o0 = 8 * sset
        stb = STbs[sset]
        s1, nmean, s2, rstd = [STv[:, o0 + q:o0 + q + 1] for q in range(4)]
        ops = []
        ops.append(lambda: P.op("dve", lambda e: e.scalar_tensor_tensor(out=junk, in0=xap, scalar=1.0, in1=zcol.to_broadcast([128, D]),
                                                                      op0=ALU.mult, op1=ALU.add, accum_out=s1), [xbuf, CONSTb], [stb]))
        ops.append(lambda: P.op("dve", lambda e: e.tensor_scalar(out=nmean, in0=s1, scalar1=-1.0 / D, scalar2=None, op0=ALU.mult),
                                [stb], [stb]))
        ops.append(lambda: P.op("dve", lambda e: e.tensor_scalar(out=xap, in0=xap, scalar1=nmean, scalar2=None, op0=ALU.add),
                                [xbuf, stb], [xbuf]))
        ops.append(lambda: P.op("dve", lambda e: e.scalar_tensor_tensor(out=junk, in0=xap, scalar=1.0, in1=xap,
                                                                      op0=ALU.mult, op1=ALU.mult, accum_out=s2), [xbuf], [stb]))
        ops.append(lambda: P.op("dve", lambda e: e.tensor_scalar(out=rstd, in0=s2, scalar1=1.0 / D, scalar2=EPS, op0=ALU.mult,
                                                                  op1=ALU.add), [stb], [stb]))
        ops.append(lambda: P.op("act", lambda e: e.activation(out=rstd, in_=rstd, func=AF.Sqrt), [stb], [stb]))
        ops.append(lambda: P.op("dve", lambda e: e.reciprocal(out=rstd, in_=rstd), [stb], [stb]))
        ops.append(lambda: P.op("dve", lambda e: e.scalar_tensor_tensor(out=xap, in0=xap, scalar=rstd, in1=LG, op0=ALU.mult,
                                                                      op1=ALU.mult), [xbuf, stb, PARb], [xbuf]))
        ops.append(lambda: P.op("dve", lambda e: e.tensor_tensor(out=xap, in0=xap, in1=LB, op=ALU.add), [xbuf, PARb], [xbuf]))
        return ops

    def zip_run(lists):
        n = max(len(l) for l in lists)
        for i in range(n):
            for l in lists:
                if i < len(l):
                    l[i]()

    def phase_M(L):
        MT = view(A_QK, [8, S])
        MTb = [Buf("mt%d" % c) for c in range(8)]
        SG = [view(A_WORK + i * 2048, [512], F32) for i in range(2)]
        SGb = [Buf("sg%d" % i) for i in range(2)]
        ACm, ACmb = view(A_WORK + 4096, [512], F32), Buf("acm")
        wg2 = wg_s.ap()[L].rearrange("(k p) c -> p k c", p=128)
        wb2 = wbr_s.ap()[L].rearrange("(k p) c -> p k c", p=128)
        chunks = {0: [0, 1, 2], 1: [3, 4], 2: [5, 6, 7]}
        sk = [0]

        def do_cc(cc):
            k = wrr[0] % 2
            wrr[0] += 1
            wgv = view(A_WR + k * 8192, [3, 8, 128])
            pbv = view(A_WR + k * 8192 + 6144, [8, 128])
            for b_ in range(3):
                P.dma(st_w[k], wgv[:, b_, :, :], wg2[:, :, b_ * 1024 + cc * 128:b_ * 1024 + cc * 128 + 128],
                      r=[WSb], w=[WRb[k]], new_batch=(b_ == 0))
            P.dma(st_w[k], pbv, wb2[:, :, cc * 128:cc * 128 + 128], r=[WSb], w=[WRb[k]], new_batch=False)

            def do_tb(tb):
                cols = slice(tb * 512, (tb + 1) * 512)
                for b_ in range(3):
                    io = nextps()
                    ch = chunks[b_]
                    for n_, kk in enumerate(ch):
                        mm(PS[io], pbv[:, kk, :], OT[:, kk, cols], n_ == 0, n_ == len(ch) - 1, [WRb[k], OTb[kk]], [PSb[io]])
                    ig = nextps()
                    for kc in range(8):
                        mm(PS[ig], wgv[:, b_, kc, :], XT[:, kc, cols], kc == 0, kc == 7, [WRb[k]] + XTb[4 * tb:4 * tb + 4], [PSb[ig]])
                    sg, sgb = SG[sk[0] % 2], SGb[sk[0] % 2]
                    sk[0] += 1
                    P.op("act", lambda e, sg=sg, ig=ig: e.activation(out=sg, in_=PS[ig], func=AF.Sigmoid), [PSb[ig]], [sgb])
                    if b_ == 0:
                        P.op("dve", lambda e, sg=sg, io=io: e.tensor_tensor(out=ACm, in0=sg, in1=PS[io], op=ALU.mult),
                             [sgb, PSb[io]], [ACmb])
                    else:
                        P.op("dve", lambda e, sg=sg, io=io: e.tensor_tensor(out=sg, in0=sg, in1=PS[io], op=ALU.mult),
                             [sgb, PSb[io]], [sgb])
                        if b_ == 1:
                            P.op("dve", lambda e, sg=sg: e.tensor_tensor(out=ACm, in0=ACm, in1=sg, op=ALU.add), [sgb, ACmb], [ACmb])
                        else:
                            P.op("dve", lambda e, sg=sg: e.tensor_tensor(out=MT[:, cc, cols], in0=ACm, in1=sg, op=ALU.add),
                                 [sgb, ACmb], [MTb[cc]])
            for tb in range(4):
                do_tb(tb)
        for cc in range(8):
            do_cc(cc)
        P.barrier()
        return MT, MTb

    def phase_R1(b, L, MT, MTb):
        X1 = view(A_VV, [NT, D], F32)
        X1b = [Buf("x1_%d" % j) for j in range(NT)]
        LG, LB = view(A_OT, [D], F32), view(A_OT + 4096, [D], F32)
        LNb = Buf("ln")
        XS = [view(A_OT + 8192 + i * 4096, [D], F32) for i in range(2)]
        XSb = [Buf("xs%d" % i) for i in range(2)]
        XB = [view(A_OT + 16384 + i * 2048, [D]) for i in range(2)]
        XBb = [Buf("xb%d" % i) for i in range(2)]
        JK = view(A_OT + 20480, [D], F32)
        rowb = Buf("lnrow")
        load_ln(LG, ln1g_h, L, JK, rowb, LNb)
        load_ln(LB, ln1b_h, L, JK, rowb, LNb)
        wo2 = wo_s.ap()[L].rearrange("(k p) c -> p k c", p=128)
        for hh in range(2):
            P.dma(st_w[hh], WR[hh], wo2[:, :, hh * 512:(hh + 1) * 512], r=[WSb], w=[WRb[hh]])

        JK2 = [JK, view(A_OT + 24576, [D], F32)]

        def stage1(j):
            t0 = 128 * j
            k = j % 2
            src = x_h.ap()[b, t0:t0 + 128, :] if L == 0 else xres_s.ap()[t0:t0 + 128, :]
            P.dma(st_x[k], XS[k], src, r=[XRb], w=[XSb[k]])
            for hh in range(2):
                i = nextps()
                for cc in range(8):
                    mm(PS[i], MT[:, cc, t0:t0 + 128], WR[hh][:, cc, :], cc == 0, cc == 7, [MTb[cc], WRb[hh]], [PSb[i]])
                P.op("dve", lambda e, i=i, hh=hh: e.scalar_tensor_tensor(
                    out=X1[:, j, hh * 512:(hh + 1) * 512], in0=XS[k][:, hh * 512:(hh + 1) * 512], scalar=ALPHA,
                    in1=PS[i], op0=ALU.mult, op1=ALU.add), [XSb[k], PSb[i]], [X1b[j]])

        def stage3(j):
            k = j % 2
            P.op("act", lambda e: e.activation(out=XB[k], in_=X1[:, j, :], func=AF.Copy), [X1b[j]], [XBb[k]])
            to_xt(j, XB[k], XBb[k])
            P.op("act", lambda e: e.activation(out=X1[:, j, :], in_=X1[:, j, :], func=AF.Copy, scale=ALPHA),
                 [X1b[j], XBb[k]], [X1b[j]])

        def do_pair(j0):
            stage1(j0)
            stage1(j0 + 1)
            zip_run([ln_ops(X1[:, j0 + q, :], LG, LB, LNb, JK2[q], X1b[j0 + q], q) for q in range(2)])
            stage3(j0)
            stage3(j0 + 1)
        for j0 in range(0, NT, 2):
            do_pair(j0)
        P.barrier()
        return X1, X1b

    def phase_F(b, L, last, X1, X1b):
        HT = view(A_QK, [8, S])
        HTb = [Buf("ht%d" % c) for c in range(8)]
        W4 = [view(A_WR, [8, 512]), view(A_WR + 8192, [8, 512]),
              view(A_WORK + 16384, [8, 512]), view(A_WORK + 24576, [8, 512])]
        W4b = [Buf("w4_%d" % i) for i in range(4)]
        w4r = [0]
        TM = [view(A_OT + i * 2048, [512], F32) for i in range(2)]
        TMb = [Buf("tm%d" % i) for i in range(2)]
        tk = [0]
        wu2 = wu_s.ap()[L].rearrange("(k p) c -> p k c", p=128)

        def wload(srcv):
            k = w4r[0] % 4
            w4r[0] += 1
            P.dma(st_w[k], W4[k], srcv, r=[WSb], w=[W4b[k]])
            return W4[k], W4b[k]

        def up_half(g, half):
            wt, wb = wload(wu2[:, :, g * 1024 + half * 512:g * 1024 + half * 512 + 512])
            for hc in range(4):
                for tb in range(4):
                    def one(hc=hc, tb=tb):
                        cols = slice(tb * 512, (tb + 1) * 512)
                        i = nextps()
                        for kc in range(8):
                            mm(PS[i], wt[:, kc, hc * 128:(hc + 1) * 128], XT[:, kc, cols], kc == 0, kc == 7,
                               [wb] + XTb[4 * tb:4 * tb + 4], [PSb[i]])
                        tm, tmb = TM[tk[0] % 2], TMb[tk[0] % 2]
                        tk[0] += 1
                        P.op("act", lambda e: e.activation(out=tm, in_=PS[i], func=AF.Relu), [PSb[i]], [tmb])
                        eng = "dve"
                        P.op(eng, lambda e: e.tensor_tensor(out=HT[:, half * 4 + hc, cols], in0=tm, in1=tm, op=ALU.mult),
                             [tmb], [HTb[half * 4 + hc]])
                    one()

        def down_half(g, h2):
            wd2 = wd_s.ap()[L][g * 1024:(g + 1) * 1024, :].rearrange("(k p) c -> p k c", p=128)
            wt, wb = wload(wd2[:, :, h2 * 512:(h2 + 1) * 512])
            for j in range(NT):
                def one(j=j):
                    i = nextps()
                    for hc in range(8):
                        mm(PS[i], HT[:, hc, j * 128:(j + 1) * 128], wt[:, hc, :], hc == 0, hc == 7, [HTb[hc], wb], [PSb[i]])
                    P.op("dve", lambda e: e.tensor_tensor(out=X1[:, j, h2 * 512:(h2 + 1) * 512],
                                                          in0=X1[:, j, h2 * 512:(h2 + 1) * 512], in1=PS[i], op=ALU.add),
                         [X1b[j], PSb[i]], [X1b[j]])
                one()

        for g in range(4):
            for half in range(2):
                up_half(g, half)
            for h2 in range(2):
                down_half(g, h2)
        P.barrier()
        wpg2 = wpg_s.ap()[L].rearrange("(k p) c -> p k c", p=128)
        WPG = []
        for hh in range(2):
            WPG.append(wload(wpg2[:, :, hh * 512:(hh + 1) * 512]))
        WPL = view(A_QK, [2, D])
        WPLb = Buf("wpl")
        P.dma(st_m, WPL, wpl_s.ap()[L].rearrange("(k p) c -> p k c", p=128), r=[WSb], w=[WPLb])
        LG, LB = view(A_QK + 4096, [D], F32), view(A_QK + 8192, [D], F32)
        LNb = Buf("ln2")
        JK = view(A_QK + 12288, [D], F32)
        rowb = Buf("lnrow2")
        load_ln(LG, ln2g_h, L, JK, rowb, LNb)
        load_ln(LB, ln2b_h, L, JK, rowb, LNb)
        PSg = [view(A_QK + 16384 + i * 1024, [256], F32) for i in range(2)]
        PSgb = [Buf("pf%d" % i) for i in range(2)]
        PBf = [view(A_QK + 18432 + i * 512, [256]) for i in range(2)]
        PBfb = [Buf("pb%d" % i) for i in range(2)]
        PTt = [view(A_QK + 19456 + i * 512, [2, 128]) for i in range(2)]
        PTtb = [Buf("ptt%d" % i) for i in range(2)]
        SG = [view(A_QK + 20480 + i * 2048, [512], F32) for i in range(2)]
        SGb = [Buf("sg%d" % i) for i in range(2)]
        XB = [view(A_QK + 24576 + i * 2048, [D]) for i in range(2)]
        XBb = [Buf("xb%d" % i) for i in range(2)]
        sk = [0]

        JK2 = [JK, view(A_QK + 28672, [D], F32)]

        def stage1(j):
            t0 = 128 * j
            k = j % 2
            P.dma(st_x[k], PSg[k], p_h.ap()[L, b, t0:t0 + 128, :], w=[PSgb[k]])
            P.op("pool", lambda e: e.tensor_copy(out=PBf[k], in_=PSg[k]), [PSgb[k]], [PBfb[k]])
            i = nextps()
            pbf = PS[i].bitcast(BF16)
            for c2 in range(2):
                P.op("pe", lambda e, c2=c2: e.transpose(pbf[:, c2 * 128:(c2 + 1) * 128], PBf[k][:, c2 * 128:(c2 + 1) * 128], ident),
                     [PBfb[k], CONSTb], [PSb[i]])
            P.op("act", lambda e: e.activation(out=PTt[k], in_=pbf[:, 0:256].rearrange("p (a b) -> p a b", a=2), func=AF.Copy),
                 [PSb[i]], [PTtb[k]])
            for hh in range(2):
                def one(hh=hh):
                    cols = slice(hh * 512, (hh + 1) * 512)
                    ig = nextps()
                    wt, wb = WPG[hh]
                    for kc in range(8):
                        mm(PS[ig], XT[:, kc, t0:t0 + 128], wt[:, kc, :], kc == 0, kc == 7, [XTb[j], wb], [PSb[ig]])
                    ip = nextps()
                    for c2 in range(2):
                        mm(PS[ip], PTt[k][:, c2, :], WPL[:, c2, cols], c2 == 0, c2 == 1, [PTtb[k], WPLb], [PSb[ip]])
                    sg, sgb = SG[sk[0] % 2], SGb[sk[0] % 2]
                    sk[0] += 1
                    P.op("act", lambda e: e.activation(out=sg, in_=PS[ig], func=AF.Sigmoid), [PSb[ig]], [sgb])
                    P.op("dve", lambda e: e.tensor_tensor(out=sg, in0=sg, in1=PS[ip], op=ALU.mult), [sgb, PSb[ip]], [sgb])
                    P.op("dve", lambda e: e.tensor_tensor(out=X1[:, j, cols], in0=X1[:, j, cols], in1=sg, op=ALU.add),
                         [sgb, X1b[j]], [X1b[j]])
                one()

        def stage3(j):
            t0 = 128 * j
            k = j % 2
            if last:
                P.dma(st_o[k], out_h.ap()[b, t0:t0 + 128, :], X1[:, j, :], r=[X1b[j]])
            else:
                P.dma(st_o[k], xres_s.ap()[t0:t0 + 128, :], X1[:, j, :], r=[X1b[j]], w=[XRb])
                P.op("act", lambda e: e.activation(out=XB[k], in_=X1[:, j, :], func=AF.Copy), [X1b[j]], [XBb[k]])
                to_xt(j, XB[k], XBb[k])

        def do_pair(j0):
            stage1(j0)
            stage1(j0 + 1)
            zip_run([ln_ops(X1[:, j0 + q, :], LG, LB, LNb, JK2[q], X1b[j0 + q], q) for q in range(2)])
            stage3(j0)
            stage3(j0 + 1)
        for j0 in range(0, NT, 2):
            do_pair(j0)
        P.barrier()

    XRb = Buf("xres")

    def layer(b, L, last):
        wi = wi_s.ap()[L]
        phase_A(wi)
        if stop == "A":
            return
        phase_C(wi)
        if stop in ("C", "C1", "C2", "C3", "C3a", "C3b"):
            return
        phase_B(wi)
        if stop in ("B", "B1"):
            return
        MT, MTb = phase_M(L)
        if stop == "M":
            return
        X1, X1b = phase_R1(b, L, MT, MTb)
        if "x1" in dbg_h:
            for j in range(NT):
                P.dma(st_m, dbg_h["x1"].ap()[j * 128:(j + 1) * 128, :], X1[:, j, :], r=[X1b[j]], new_batch=(j == 0))
            P.barrier()
        if stop == "R1":
            return
        phase_F(b, L, last, X1, X1b)

    for b in range(NB):
        def xsrc(j, b=b):
            return x_h.ap()[b, j * 128:(j + 1) * 128, :]
        load_x_tiles(xsrc, 0)
        P.barrier()
        for L in range(NL):
            layer(b, L, L == NL - 1)
            if stop is not None:
                break
            P.new_epoch()
        if stop is not None:
            break
    if "ot_all" in dbg_h:
        P.dma(st_m, dbg_h["ot_all"].ap(), OT, r=OTb)
    P.barrier()
    P.emit()
    return nc


def kernel(**inputs):
    NCORE, NB = 8, 4
    cb, cf, oh, ohb = make_consts()
    nc = build(NB, 2)
    names = ["w_in", "w_gate", "w_br_a", "w_br_b", "w_br_c", "w_out", "w_up", "w_down", "w_ple_gate",
             "w_ple", "ln1_g", "ln1_b", "ln2_g", "ln2_b", "rel_bias"]
    shared = {k: np.ascontiguousarray(np.asarray(inputs[k], dtype=np.float32)) for k in names}
    shared.update({"c_bf": cb, "c_f32": cf, "c_oh": oh, "c_ohb": ohb})
    x = np.asarray(inputs["x"], dtype=np.float32)
    p = np.asarray(inputs["p"], dtype=np.float32)
    in_maps = []
    for c in range(NCORE):
        m = dict(shared)
        m["x"] = np.ascontiguousarray(x[c * NB:(c + 1) * NB])
        m["p"] = np.ascontiguousarray(p[:, c * NB:(c + 1) * NB])
        in_maps.append(m)
    res = run_bass_kernel_spmd(nc, in_maps, core_ids=list(range(NCORE)))
    return np.concatenate([np.asarray(r["out"], dtype=np.float32) for r in res.results], axis=0)
```

```python
import math
import numpy as np
import ml_dtypes
import concourse.bass as bass
import concourse.mybir as mybir
from concourse.bass_utils import run_bass_kernel_spmd

F32 = mybir.dt.float32
BF16 = mybir.dt.bfloat16
AF = mybir.ActivationFunctionType
ALU = mybir.AluOpType
AX = mybir.AxisListType

S = 2048
D = 1024
NT = 16
BIG = 30000.0
NIT = 18
FUSE_WAIT = True
ALPHA = 4 ** 0.25
EPS = 1e-5
WIC = 3656
FW = 2304
FWB = 512


class Buf:
    __slots__ = ("name", "w", "rs")
    ALL = []

    def __init__(self, name=""):
        Buf.ALL.append(self)
        self.name = name
        self.w = None
        self.rs = {}


class Stream:
    def __init__(self, sem):
        self.sem = sem
        self.count = 0


class Op:
    __slots__ = ("eng", "fn", "idx", "ewait", "swait", "stream", "epoch")

    def __init__(self, eng, fn):
        self.epoch = 0
        self.eng = eng
        self.fn = fn
        self.idx = 0
        self.ewait = {}
        self.swait = {}
        self.stream = None


class Prog:
    def __init__(self, nc):
        self.nc = nc
        self.e = {"pe": nc.tensor, "act": nc.scalar, "dve": nc.vector,
                  "pool": nc.gpsimd, "sp": nc.sync}
        self.ops = {k: [] for k in self.e}
        self.cnt = {k: 0 for k in self.e}
        self.sems = [{k: nc.alloc_semaphore("sem_" + k) for k in self.e}]
        self.epoch = 0
        self.streams = []

    def stream(self):
        s = Stream(self.nc.alloc_semaphore("dsem%d" % len(self.streams)))
        self.streams.append(s)
        return s

    INORDER = ("dve", "act")
    RAW_DIST = 3

    def _dep_op(self, o, d, raw=True):
        if d is None or d is o:
            return
        if d.stream is not None:
            s = d.stream
            o.swait[s] = max(o.swait.get(s, 0), s.count)
        else:
            if d.eng == "pe" and o.eng == "pe":
                return
            if d.eng == o.eng and d.eng in self.INORDER and d.epoch == o.epoch:
                if (not raw) or (o.idx - d.idx >= self.RAW_DIST):
                    return
            o.ewait[d.eng] = max(o.ewait.get(d.eng, 0), d.idx + 1)

    def op(self, eng, fn, r=(), w=(), stream=None):
        o = Op(eng, fn)
        o.epoch = self.epoch
        o.idx = self.cnt[eng]
        if stream is None:
            self.cnt[eng] += 1
        for b in r:
            self._dep_op(o, b.w)
        for b in w:
            self._dep_op(o, b.w, raw=False)
            for k, v in b.rs.items():
                if isinstance(k, Stream):
                    o.swait[k] = max(o.swait.get(k, 0), k.count)
                else:
                    if k == eng and (k == "pe" or k in self.INORDER):
                        continue
                    o.ewait[k] = max(o.ewait.get(k, 0), v)
        if stream is not None:
            o.stream = stream
        for b in r:
            if stream is not None:
                b.rs[stream] = 1
            else:
                b.rs[eng] = o.idx + 1
        for b in w:
            b.w = o
            b.rs = {}
        self.ops[eng].append(o)
        return o

    def dma(self, stream, out, in_, r=(), w=(), new_batch=True, eng="sp"):
        o = self.op(eng, lambda e: e.dma_start(out=out, in_=in_), r, w, stream=stream)
        if new_batch and stream.count > 0:
            o.swait[stream] = max(o.swait.get(stream, 0), stream.count)
        stream.count += 1
        return o

    def barrier(self):
        last = dict(self.cnt)
        for k in self.e:
            o = Op(k, lambda e: e.nop())
            o.epoch = self.epoch
            o.idx = self.cnt[k]
            self.cnt[k] += 1
            for k2, n in last.items():
                if n > 0:
                    o.ewait[k2] = n
            for s in self.streams:
                if s.count > 0:
                    o.swait[s] = s.count
            self.ops[k].append(o)

    def new_epoch(self):
        self.epoch += 1
        self.sems.append({k: self.nc.alloc_semaphore("sem%d_%s" % (self.epoch, k)) for k in self.e})
        self.cnt = {k: 0 for k in self.e}
        for b in Buf.ALL:
            b.w = None
            b.rs = {}

    def emit(self):
        nc = self.nc
        with nc.Block() as block:
            def run(k, eng):
                waited = {}
                ep = 0
                for o in self.ops[k]:
                    if o.epoch != ep:
                        ep = o.epoch
                        waited = {kk: vv for kk, vv in waited.items() if isinstance(kk, Stream)}
                    pend = []
                    for ek, v in o.ewait.items():
                        if waited.get(ek, 0) < v:
                            pend.append((self.sems[ep][ek], v))
                            waited[ek] = v
                    for s, c in o.swait.items():
                        if waited.get(s, 0) < c:
                            pend.append((s.sem, 16 * c))
                            waited[s] = c
                    fuse = None
                    if pend and FUSE_WAIT and k != "sp" and o.stream is None:
                        fuse = pend.pop()
                    for (sm, v) in pend:
                        eng.wait_ge(sm, v)
                    ins = o.fn(eng)
                    if fuse is not None:
                        ins._wait_ge(fuse[0], fuse[1])
                    if o.stream is not None:
                        ins.then_inc(o.stream.sem, 16)
                    else:
                        ins.then_inc(self.sems[ep][k], 1)

            @block.tensor
            def _(e):
                run("pe", e)

            @block.scalar
            def _(e):
                run("act", e)

            @block.vector
            def _(e):
                run("dve", e)

            @block.gpsimd
            def _(e):
                run("pool", e)

            @block.sync
            def _(e):
                run("sp", e)


def _rel_bucket(d):
    n = np.maximum(d, 0)
    me = 16
    nf = np.maximum(n, 1).astype(np.float32)
    large = me + (np.log(nf / np.float32(me)) / np.float32(math.log(2048 / me))
                  * np.float32(32 - me)).astype(np.int32)
    large = np.minimum(large, 31)
    return np.where(n < me, n, large)


CB_ID, CB_J, CB_ID3, CB_E8, CB_N = 0, 128, 256, 640, 1664
CF_SWAP, CF_CAUS, CF_POW, CF_CM, CF_Z, CF_N = 0, 128, 256, 288, 672, 704


def make_consts():
    cb = np.zeros((128, CB_N), np.float32)
    cb[:, CB_ID:CB_ID + 128] = np.eye(128)
    cb[:, CB_J:CB_J + 128] = np.eye(128)[::-1]
    for a in range(3):
        cb[:, CB_ID3 + 128 * a:CB_ID3 + 128 * (a + 1)] = np.eye(128)
    for n in range(8):
        cb[n, CB_E8 + 128 * n:CB_E8 + 128 * (n + 1)] = 1.0
    cf = np.zeros((128, CF_N), np.float32)
    sw = np.zeros((128, 128), np.float32)
    for i in range(128):
        sw[i, (i + 64) % 128] = 1.0
    cf[:, CF_SWAP:CF_SWAP + 128] = sw
    tl = np.arange(128)[:, None]
    sl = np.arange(128)[None, :]
    cf[:, CF_CAUS:CF_CAUS + 128] = np.where(sl > tl, -1e30, 0.0)
    for k in range(32):
        cf[:, CF_POW + k] = 2.0 ** (-(k + 1))
    for cur in range(8):
        m = np.zeros((6, 8), np.float32)
        m[:, cur:] = -1e30
        cf[:, CF_CM + 48 * cur:CF_CM + 48 * (cur + 1)] = m.reshape(-1)
    oh = np.zeros((33, FW), np.float32)
    for i in range(FW):
        d = i - 127
        if d < 0 or d > 2047:
            oh[32, i] = 1.0
        else:
            oh[int(_rel_bucket(np.array([d]))[0]), i] = 8.0
    ohb = np.zeros((33, 3 * FWB), np.float32)
    for g, dil in enumerate((1, 4, 16)):
        for i in range(FWB):
            dl = i - 127
            if dl < 0 or dl > 128:
                ohb[32, g * FWB + i] = 1.0
            else:
                ohb[int(_rel_bucket(np.array([dl * dil]))[0]), g * FWB + i] = 8.0
    return (cb.astype(ml_dtypes.bfloat16), cf, oh, ohb)


A_CONST = 0
A_XT = 10240
A_OT = A_XT + 32768
A_WR = A_OT + 32768
A_QK = A_WR + 16384
A_VV = A_QK + 36864
A_TAB = A_VV + 25088
A_WORK = A_TAB + 24576
A_END = A_WORK + 32768
ROW = A_END // 2


def build(NB, NL, dbg=(), stop=None):
    nc = bass.Bass("TRN2", target_bir_lowering=False)
    Buf.ALL = []
    P = Prog(nc)

    def din(name, shape, dt=F32):
        return nc.dram_tensor(name, list(shape), dt, kind="ExternalInput")

    x_h = din("x", [NB, S, D])
    p_h = din("p", [2, NB, S, 256])
    w_in_h = din("w_in", [2, D, 3528])
    w_gate_h = din("w_gate", [2, D, 3072])
    w_bra_h = din("w_br_a", [2, 384, D])
    w_brb_h = din("w_br_b", [2, 256, D])
    w_brc_h = din("w_br_c", [2, 384, D])
    w_out_h = din("w_out", [2, D, D])
    w_up_h = din("w_up", [2, D, 4096])
    w_down_h = din("w_down", [2, 4096, D])
    w_pg_h = din("w_ple_gate", [2, D, D])
    w_ple_h = din("w_ple", [2, 256, D])
    ln1g_h = din("ln1_g", [2, D])
    ln1b_h = din("ln1_b", [2, D])
    ln2g_h = din("ln2_g", [2, D])
    ln2b_h = din("ln2_b", [2, D])
    relb_h = din("rel_bias", [32, 24])
    cb_h = din("c_bf", [128, CB_N], BF16)
    cf_h = din("c_f32", [128, CF_N])
    oh_h = din("c_oh", [33, FW])
    ohb_h = din("c_ohb", [33, 3 * FWB])
    out_h = nc.dram_tensor("out", [NB, S, D], F32, kind="ExternalOutput")
    dbg_h = {}
    for name, shape, dt in dbg:
        dbg_h[name] = nc.dram_tensor("dbg_" + name, list(shape), dt, kind="ExternalOutput")

    wi_s = nc.dram_tensor("wi_s", [2, D, WIC], BF16)
    wg_s = nc.dram_tensor("wg_s", [2, D, 3072], BF16)
    wbr_s = nc.dram_tensor("wbr_s", [2, 1024, D], BF16)
    wo_s = nc.dram_tensor("wo_s", [2, D, D], BF16)
    wu_s = nc.dram_tensor("wu_s", [2, D, 4096], BF16)
    wd_s = nc.dram_tensor("wd_s", [2, 4096, D], BF16)
    wpg_s = nc.dram_tensor("wpg_s", [2, D, D], BF16)
    wpl_s = nc.dram_tensor("wpl_s", [2, 256, D], BF16)
    xres_s = nc.dram_tensor("xres_s", [S, D], F32)
    frow_s = nc.dram_tensor("frow_s", [24, FW], BF16)
    frowb_s = nc.dram_tensor("frowb_s", [3, 24, FWB], BF16)
    eg_s = nc.dram_tensor("eg_s", [2, 128, 6 * S], BF16)

    arena = nc.alloc_sbuf_tensor("arena", [128, ROW], BF16)

    def view(off, shape, dt=BF16):
        n = int(np.prod(shape))
        esz = 2 if dt == BF16 else 4
        a = arena[:, off // 2: off // 2 + n * esz // 2]
        if dt == F32:
            a = a.bitcast(F32)
        if len(shape) == 2:
            a = a.rearrange("p (a b) -> p a b", a=shape[0], b=shape[1])
        elif len(shape) == 3:
            a = a.rearrange("p (a b c) -> p a b c", a=shape[0], b=shape[1], c=shape[2])
        return a

    def raw(off_el, dims, p0=0, np_=128):
        return bass.AP(arena, p0 * ROW + off_el, [[ROW, np_]] + [list(d) for d in dims])

    PSH = [nc.alloc_psum_tensor("ps%d" % i, [128, 512], F32) for i in range(8)]
    PS = [h[:, :] for h in PSH]
    PSb = [Buf("ps%d" % i) for i in range(8)]

    cbv = view(A_CONST, [CB_N])
    cfv = view(A_CONST + 3328, [CF_N], F32)
    TBv = view(A_CONST + 6144, [24], F32)
    IWv = view(A_CONST + 6240, [NT, 8], F32)
    SMv = view(A_CONST + 6752, [64], F32)
    STv = view(A_CONST + 7008, [16], F32)
    KMf = view(A_CONST + 7072, [3, 8], F32)
    KMp = view(A_CONST + 7168, [3, 48])
    GSB = view(A_CONST + 7456, [48], F32)
    M8 = view(A_CONST + 7648, [6, 8], F32)
    NSEL = view(A_CONST + 7840, [6, 8])
    ONESf = view(A_CONST + 9728, [128], F32)
    CONSTb = Buf("const")
    ident = cbv[:, CB_ID:CB_ID + 128]
    Jm = cbv[:, CB_J:CB_J + 128]
    id3 = cbv[:, CB_ID3:CB_ID3 + 384]
    swapm = cfv[:, CF_SWAP:CF_SWAP + 128]
    causn = cfv[:, CF_CAUS:CF_CAUS + 128]
    zcol = cfv[:, CF_Z:CF_Z + 1]

    XT = view(A_XT, [8, S])
    XTb = [Buf("xt%d" % j) for j in range(NT)]
    OT = view(A_OT, [8, S])
    OTb = [Buf("ot%d" % c) for c in range(8)]
    QK = view(A_QK, [9, S])
    QKb = [Buf("qk%d" % c) for c in range(9)]
    VVb = Buf("vv")
    TABb = Buf("tab")
    VV_EL = A_VV // 2
    OA_EL = VV_EL + 64 + 12288

    st_c = P.stream()
    st_w = [P.stream() for _ in range(4)]
    st_x = [P.stream() for _ in range(2)]
    st_o = [P.stream() for _ in range(2)]
    st_t = P.stream()
    st_m = P.stream()

    psrr = [0]

    def nextps():
        i = psrr[0] % 8
        psrr[0] += 1
        return i

    evrr = [0]

    def evac(out, in_, r, w, engs=("act", "dve")):
        k = engs[evrr[0] % len(engs)]
        evrr[0] += 1
        if k == "act":
            P.op("act", lambda e: e.activation(out=out, in_=in_, func=AF.Copy), r, w)
        elif k == "dve":
            P.op("dve", lambda e: e.tensor_copy(out=out, in_=in_), r, w)
        else:
            P.op("pool", lambda e: e.tensor_copy(out=out, in_=in_), r, w)

    def mm(out, lhsT, rhs, start, stop, r, w):
        P.op("pe", lambda e: e.matmul(out, lhsT, rhs, start=start, stop=stop), r, w)

    def dump(name, src, r):
        if name in dbg_h:
            P.dma(st_m, dbg_h[name].ap(), src, r=r)

    P.dma(st_c, cbv, cb_h.ap(), w=[CONSTb])
    P.dma(st_c, cfv, cf_h.ap(), w=[CONSTb], new_batch=False)
    P.dma(st_c, TBv[0:32, :], relb_h.ap(), w=[CONSTb], new_batch=False)
    P.op("dve", lambda e: e.memset(TBv[32:33, :], -BIG), w=[CONSTb])
    P.op("dve", lambda e: e.memset(ONESf[0:1, :], 1.0), w=[CONSTb])

    ohv = view(A_WORK, [FW], F32)
    ohbv = view(A_WORK + 4 * FW, [3 * FWB], F32)
    frs = view(A_WORK + 4 * FW + 12 * FWB, [FW])
    frbs = view(A_WORK + 6 * FW + 12 * FWB, [3 * FWB])
    Wb = Buf("work")
    P.dma(st_m, ohv[0:33, :], oh_h.ap(), w=[Wb])
    P.dma(st_m, ohbv[0:33, :], ohb_h.ap(), w=[Wb], new_batch=False)
    for c0 in range(0, FW, 512):
        wd_ = min(512, FW - c0)
        i = nextps()
        mm(PS[i][0:24, :wd_], TBv[0:33, :], ohv[0:33, c0:c0 + wd_], True, True, [CONSTb, Wb], [PSb[i]])
        evac(frs[0:24, c0:c0 + wd_], PS[i][0:24, :wd_], [PSb[i]], [Wb], engs=("dve",))
    for c0 in range(0, 3 * FWB, 512):
        i = nextps()
        mm(PS[i][0:24, :512], TBv[0:33, :], ohbv[0:33, c0:c0 + 512], True, True, [CONSTb, Wb], [PSb[i]])
        evac(frbs[0:24, c0:c0 + 512], PS[i][0:24, :512], [PSb[i]], [Wb], engs=("dve",))
    FRb = Buf("frow")
    P.dma(st_m, frow_s.ap(), frs[0:24, :], r=[Wb], w=[FRb])
    for g in range(3):
        P.dma(st_m, frowb_s.ap()[g], frbs[0:24, g * FWB:(g + 1) * FWB], r=[Wb], w=[FRb],
              new_batch=False)
    P.barrier()
    Gset = view(A_TAB, [6 * S])
    EGst = view(A_QK, [6 * S])
    EGb = Buf("egst")
    for tbl, h0 in ((1, 18),):
        P.dma(st_t, Gset.rearrange("p (a b) -> p a b", a=6), bass.AP(frow_s, h0 * FW, [[1, 128], [FW, 6], [1, S]]),
              r=[FRb], w=[TABb])
        for ci in range(24):
            def one(ci=ci):
                i = nextps()
                mm(PS[i], Jm, Gset[:, ci * 512:(ci + 1) * 512], True, True, [CONSTb, TABb], [PSb[i]])
                P.op("act", lambda e: e.activation(out=EGst[:, ci * 512:(ci + 1) * 512], in_=PS[i], func=AF.Exp, scale=0.125),
                     [PSb[i]], [EGb])
            one()
        P.dma(st_m, eg_s.ap()[tbl], EGst, r=[EGb], w=[FRb])
    P.barrier()
    if stop == "setup":
        P.emit()
        return nc

    WSb = Buf("wscratch")
    NSLOT = 4
    stg = [view(A_XT + i * 16384, [4096], F32) for i in range(NSLOT)]
    stgb = [Buf("stg%d" % i) for i in range(NSLOT)]
    cst = [view(A_QK + i * 8192, [4096]) for i in range(NSLOT)]
    cstb = [Buf("cst%d" % i) for i in range(NSLOT)]
    st_pl = [P.stream() for _ in range(NSLOT)]
    st_ps = [P.stream() for _ in range(NSLOT)]
    pc = [0]
    WI_PIECES = [(0, 448, 0), (384, 448, 448), (512, 1088, 512), (1024, 1088, 1088),
                 (1096, 2120, 1152), (2376, 3144, 2176), (448, 512, 2944), (1088, 1096, 3008),
                 (2120, 2376, 3016), (3144, 3528, 3272)]

    def precast(src, dst, C, C2, pieces=None):
        R = src.shape[0]
        for rc in range(R // 128):
            k = pc[0] % NSLOT
            pc[0] += 1
            P.dma(st_pl[k], stg[k][:, :C], src[rc * 128:(rc + 1) * 128, :], w=[stgb[k]])
            eng = ("dve", "pool", "dve")[pc[0] % 3]
            for (s0, s1, d0) in (pieces or [(0, C, 0)]):
                o_, i_ = cst[k][:, d0:d0 + s1 - s0], stg[k][:, s0:s1]
                P.op(eng, lambda e, o_=o_, i_=i_: e.tensor_copy(out=o_, in_=i_), [stgb[k]], [cstb[k]])
            P.dma(st_ps[k], dst[rc * 128:(rc + 1) * 128, :], cst[k][:, :C2], r=[cstb[k]], w=[WSb], eng="act")

    for L in range(NL):
        precast(w_in_h.ap()[L], wi_s.ap()[L], 3528, WIC, WI_PIECES)
        for hh in range(2):
            precast(w_gate_h.ap()[L][:, hh * 1536:(hh + 1) * 1536], wg_s.ap()[L][:, hh * 1536:(hh + 1) * 1536], 1536, 1536)
        precast(w_bra_h.ap()[L], wbr_s.ap()[L][0:384], D, D)
        precast(w_brb_h.ap()[L], wbr_s.ap()[L][384:640], D, D)
        precast(w_brc_h.ap()[L], wbr_s.ap()[L][640:1024], D, D)
        precast(w_out_h.ap()[L], wo_s.ap()[L], D, D)
        for hh in range(2):
            precast(w_up_h.ap()[L][:, hh * 2048:(hh + 1) * 2048], wu_s.ap()[L][:, hh * 2048:(hh + 1) * 2048], 2048, 2048)
        precast(w_down_h.ap()[L], wd_s.ap()[L], D, D)
        precast(w_pg_h.ap()[L], wpg_s.ap()[L], D, D)
        precast(w_ple_h.ap()[L], wpl_s.ap()[L], D, D)
    P.barrier()
    if stop == "precast":
        P.emit()
        return nc

    WR = [view(A_WR + i * 8192, [8, 512]) for i in range(2)]
    WRb = [Buf("wr%d" % i) for i in range(2)]
    wrr = [0]

    def wtile(src2d, c0, ncols):
        k = wrr[0] % 2
        wrr[0] += 1
        srcv = src2d.rearrange("(k p) c -> p k c", p=128)[:, :, c0:c0 + ncols]
        P.dma(st_w[k], WR[k][:, :, :ncols], srcv, r=[WSb], w=[WRb[k]])
        return WR[k], WRb[k]

    def proj_fm(wsrc, col0, nchunks, qk0):
        for g0 in range(0, nchunks, 4):
            n = min(4, nchunks - g0)
            wt, wb = wtile(wsrc, col0 + g0 * 128, n * 128)
            for ci in range(n):
                for tb in range(4):
                    i = nextps()
                    for kc in range(8):
                        mm(PS[i], wt[:, kc, ci * 128:(ci + 1) * 128], XT[:, kc, tb * 512:(tb + 1) * 512],
                           kc == 0, kc == 7, [wb] + XTb[4 * tb:4 * tb + 4], [PSb[i]])
                    evac(QK[:, qk0 + g0 + ci, tb * 512:(tb + 1) * 512], PS[i], [PSb[i]], [QKb[qk0 + g0 + ci]])

    def proj_tm(wsrc, col0, ncols, tok_fn, ntiles, dst_fn):
        wt, wb = wtile(wsrc, col0, ncols)
        for j in range(ntiles):
            i = nextps()
            st, step = tok_fn(j)
            for kc in range(8):
                mm(PS[i][:, :ncols], XT[:, kc, st:st + 127 * step + 1:step], wt[:, kc, :ncols],
                   kc == 0, kc == 7, [wb] + XTb, [PSb[i]])
            dst_fn(j, PS[i][:, :ncols], PSb[i])

    def vlhs(hf, vcol):
        if hf == 0:
            return raw(vcol, [[1, 128]])
        return raw(vcol - 64, [[1, 128]])

    def load_x_tiles(src_fn, L):
        xs = [view(A_WORK + i * 4096, [D], F32) for i in range(2)]
        xsb = [Buf("xs%d" % i) for i in range(2)]
        xb = [view(A_WORK + 8192 + i * 2048, [D]) for i in range(2)]
        xbb = [Buf("xb%d" % i) for i in range(2)]
        for j in range(NT):
            k = j % 2
            P.dma(st_x[k], xs[k], src_fn(j), w=[xsb[k]])
            evac(xb[k], xs[k], [xsb[k]], [xbb[k]], engs=("act",))
            to_xt(j, xb[k], xbb[k])

    def to_xt(j, xb_ap, xb_buf):
        for half in range(2):
            i = nextps()
            pbf = PS[i].bitcast(BF16)
            for q in range(4):
                kc = half * 4 + q
                P.op("pe", lambda e, o_=pbf[:, q * 128:(q + 1) * 128], i_=xb_ap[:, kc * 128:(kc + 1) * 128]:
                     e.transpose(o_, i_, ident), [xb_buf, CONSTb], [PSb[i]])
            evac(XT[:, half * 4:half * 4 + 4, j * 128:(j + 1) * 128],
                 pbf[:, 0:512].rearrange("p (a b) -> p a b", a=4), [PSb[i]], [XTb[j]])

    def finish_heads(psO_i, rows_list, dst_list, ncols, osb, osbb):
        P.op("act", lambda e: e.activation(out=osb[:, :ncols], in_=PS[psO_i][:, :ncols], func=AF.Copy),
             [PSb[psO_i]], [osbb])
        i = nextps()
        mm(PS[i][:, :ncols], swapm, osb[:, :ncols], True, True, [CONSTb, osbb], [PSb[i]])
        for (rows, c0, c1, dst, dbuf) in zip(*[rows_list] * 1, *[[]] * 0) if False else []:
            pass
        for (rows, c0, c1, dst, dbuf) in dst_list:
            P.op("dve", lambda e, rows=rows, c0=c0, c1=c1, dst=dst:
                 e.tensor_tensor(out=dst, in0=osb[rows, c0:c1], in1=PS[i][rows, c0:c1], op=ALU.divide),
                 [osbb, PSb[i]], [dbuf])

    def load_tab(src_ap, dst_view):
        P.dma(st_t, dst_view, src_ap, r=[FRb], w=[TABb])

    def attn_finish(oi, ncols, osb, osbb, rd, rdb, hf, dsts, fin_bank=7):
        rows = slice(64 * hf, 64 * hf + 64)
        P.op("act", lambda e: e.activation(out=osb[:, :ncols], in_=PS[oi][:, :ncols], func=AF.Copy), [PSb[oi]], [osbb])
        fb = fin_bank
        mm(PS[fb][:, :ncols], swapm, osb[:, :ncols], True, True, [CONSTb, osbb], [PSb[fb]])
        P.op("dve", lambda e: e.reciprocal(out=rd[rows, :ncols], in_=PS[fb][rows, :ncols]), [PSb[fb]], [rdb])
        P.op("dve", lambda e: e.tensor_tensor(out=rd[rows, :ncols], in0=osb[rows, :ncols], in1=rd[rows, :ncols],
                                              op=ALU.mult), [osbb, rdb], [rdb])
        for (c0, dst, dbuf) in dsts:
            P.op("act", lambda e, c0=c0, dst=dst: e.activation(out=dst, in_=rd[rows, c0:c0 + 128], func=AF.Copy),
                 [rdb], [dbuf])

    def mmg(out, lhsT, rhs, start, stop, r, w):
        P.op("pe", lambda e: e.matmul(out, lhsT, rhs, start=start, stop=stop, skip_group_check=True), r, w)

    def run_pipe(steps, look=2):
        n = len(steps)
        if n == 0:
            return

        def doL(i):
            if steps[i][0]:
                steps[i][0]()
            steps[i][1]()
        for i in range(min(look, n)):
            doL(i)
        for i in range(n):
            if i + look < n:
                doL(i + look)
            steps[i][2]()
            steps[i][3]()

    def phase_A(wi):
        proj_fm(wi, 0, 9, 0)
        P.op("dve", lambda e: e.memset(raw(VV_EL, [[192, 16], [1, 64]]), 1.0), w=[VVb])
        P.op("dve", lambda e: e.memset(raw(VV_EL + 128, [[192, 16], [1, 64]]), 1.0), w=[VVb])

        def a_dst(j, ps, psb):
            vc = raw(VV_EL + 64 + 192 * j, [[1, 64]])
            P.op("act", lambda e: e.activation(out=vc, in_=ps[:, 0:64], func=AF.Copy), [psb], [VVb])
            P.op("dve", lambda e: e.tensor_copy(out=IWv[:, j, :], in_=ps[:, 64:72]), [psb], [CONSTb])
        proj_tm(wi, 2944, 72, lambda j: (128 * j, 1), NT, a_dst)
        P.barrier()
        G = view(A_TAB, [6, S])
        load_tab(bass.AP(frow_s, 0, [[1, 128], [FW, 6], [1, S]]), G)
        Ibuf = view(A_WORK, [S], F32)
        Ib = Buf("I")
        NM = [view(A_WORK + 8192 + i * 4096, [S]) for i in range(2)]
        NMb = [Buf("nm%d" % i) for i in range(2)]
        Dg = [view(A_WORK + 16384 + i * 2048, [8, 128]) for i in range(2)]
        Dgb = [Buf("dg%d" % i) for i in range(2)]
        Rr = [view(A_WORK + 20480 + i * 1024, [512]) for i in range(3)]
        Rb = [Buf("r%d" % i) for i in range(3)]
        PT = [view(A_WORK + 23552 + i * 768, [384]) for i in range(4)]
        PTb = [Buf("pt%d" % i) for i in range(4)]
        Osb = [view(A_WORK + 26624 + i * 1536, [384], F32) for i in range(2)]
        Osbb = [Buf("osb%d" % i) for i in range(2)]
        Rd = [view(A_WORK + 29696 + i * 1536, [384], F32) for i in range(2)]
        Rdb = [Buf("rd%d" % i) for i in range(2)]
        SMb = Buf("sm")
        rk = [0]
        pk = [0]

        Ibufs = [Ibuf, view(A_VV + 8192, [S], F32)]
        Ibs = [Ib, Buf("I2")]
        NITA = 14
        lring = [3, 4, 7]
        lk = [0]

        NM4 = [NM[0], NM[1], view(A_VV + 16384, [S]), view(A_VV + 20480, [S])]
        NM4b = [NMb[0], NMb[1], Buf("nm2"), Buf("nm3")]
        SMbs = [Buf("sm0"), Buf("sm1")]

        def idx_only(j):
            t0 = 128 * j
            N = t0 + 128
            Iv, Ivb = Ibufs[j % 2], Ibs[j % 2]
            dg, dgb = Dg[j % 2], Dgb[j % 2]
            for h in range(8):
                P.op("act", lambda e, h=h: e.activation(out=dg[:, h, :], in_=ident, func=AF.Copy, scale=IWv[:, j, h:h + 1]),
                     [CONSTb], [dgb])
            for sb in range((N + 511) // 512):
                c0 = 512 * sb
                ws = min(512, N - c0)

                def S_(h, c0=c0, ws=ws):
                    rows = slice(64 * (h % 2), 64 * (h % 2) + 64)
                    si = h % 2
                    mm(PS[si][:, :ws], QK[rows, 4 + h // 2, t0:t0 + 128], QK[rows, 8, c0:c0 + ws], True, True,
                       [QKb[4 + h // 2], QKb[8]], [PSb[si]])

                def RD_(h, c0=c0, ws=ws):
                    si = h % 2
                    rr, rb = Rr[rk[0] % 3], Rb[rk[0] % 3]
                    rk[0] += 1
                    P.op("act", lambda e: e.activation(out=rr[:, :ws], in_=PS[si][:, :ws], func=AF.Relu), [PSb[si]], [rb])
                    mm(PS[2][:, :ws], dg[:, h, :], rr[:, :ws], h == 0, h == 7, [dgb, rb], [PSb[2]])
                S_(0)
                for h in range(8):
                    if h + 1 < 8:
                        S_(h + 1)
                    RD_(h)
                P.op("dve", lambda e, c0=c0, ws=ws: e.tensor_copy(out=Iv[:, c0:c0 + ws], in_=PS[2][:, :ws]), [PSb[2]], [Ivb])
            P.op("dve", lambda e: e.tensor_tensor(out=Iv[:, t0:N], in0=Iv[:, t0:N], in1=causn, op=ALU.add), [Ivb, CONSTb], [Ivb])

        def bis_ops(j):
            if j < 2:
                return []
            t0 = 128 * j
            N = t0 + 128
            Iv, Ivb = Ibufs[j % 2], Ibs[j % 2]
            nm, nmb = NM4[j % 4], NM4b[j % 4]
            smb = SMbs[j % 2]
            o0 = 32 * (j % 2)
            hi, lo, wd0, mid, cnt, tt_ = [SMv[:, o0 + q:o0 + q + 1] for q in range(6)]
            WD = SMv[:, o0 + 8:o0 + 8 + NITA + 1]
            ops = []
            ops.append(lambda: P.op("dve", lambda e: e.tensor_reduce(out=hi, in_=Iv[:, :N], axis=AX.X, op=ALU.max), [Ivb], [smb]))
            ops.append(lambda: P.op("dve", lambda e: e.tensor_reduce(out=lo, in_=Iv[:, :t0], axis=AX.X, op=ALU.min), [Ivb], [smb]))
            ops.append(lambda: P.op("dve", lambda e: e.tensor_tensor(out=wd0, in0=hi, in1=lo, op=ALU.subtract), [smb], [smb]))
            ops.append(lambda: P.op("dve", lambda e: e.tensor_scalar(out=WD, in0=cfv[:, CF_POW:CF_POW + NITA + 1], scalar1=wd0,
                                                                      scalar2=None, op0=ALU.mult), [smb, CONSTb], [smb]))
            ops.append(lambda: P.op("dve", lambda e: e.tensor_tensor(out=mid, in0=lo, in1=WD[:, 0:1], op=ALU.add), [smb], [smb]))
            for k in range(NITA):
                ops.append(lambda: P.op("dve", lambda e: e.scalar_tensor_tensor(
                    out=nm[:, :N], in0=Iv[:, :N], scalar=mid, in1=zcol.to_broadcast([128, N]), op0=ALU.is_ge, op1=ALU.add,
                    accum_out=cnt), [Ivb, smb, CONSTb], [nmb, smb]))
                ops.append(lambda: P.op("dve", lambda e: e.tensor_scalar(out=tt_, in0=cnt, scalar1=256.0, scalar2=0.5, op0=ALU.is_ge,
                                                                          op1=ALU.subtract), [smb], [smb]))
                ops.append(lambda k=k: P.op("dve", lambda e: e.scalar_tensor_tensor(out=mid, in0=tt_, scalar=WD[:, k:k + 1], in1=mid,
                                                                                    op0=ALU.mult, op1=ALU.add), [smb], [smb]))
            ops.append(lambda: P.op("dve", lambda e: e.tensor_tensor(out=lo, in0=mid, in1=WD[:, NITA:NITA + 1], op=ALU.subtract),
                                    [smb], [smb]))
            ops.append(lambda: P.op("dve", lambda e: e.tensor_scalar(out=nm[:, :N], in0=Iv[:, :N], scalar1=lo, scalar2=-BIG,
                                                                      op0=ALU.is_lt, op1=ALU.mult), [Ivb, smb], [nmb]))
            return ops

        def idx_bis_pair(j0):
            idx_only(j0)
            idx_only(j0 + 1)
            oa, ob = bis_ops(j0), bis_ops(j0 + 1)
            for i in range(max(len(oa), len(ob))):
                if i < len(oa):
                    oa[i]()
                if i < len(ob):
                    ob[i]()

        def att_steps(j):
            t0 = 128 * j
            nm, nmb = NM4[j % 4], NM4b[j % 4]
            steps = []
            for c in range(j + 1):
                for hf in range(2):
                    def mk(c=c, hf=hf):
                        s0 = 128 * c
                        w = t0 - s0
                        rows = slice(64 * hf, 64 * hf + 64)
                        st = {}

                        def L():
                            li = lring[lk[0] % 3]
                            lk[0] += 1
                            st["li"] = li
                            pl3 = PS[li][:, :384].rearrange("p (a b) -> p a b", a=3)
                            mm(pl3, QK[rows, 3, s0:s0 + 128], QK[rows, 0:3, t0:t0 + 128], True, False,
                               [QKb[0], QKb[1], QKb[2], QKb[3]], [PSb[li]])
                            mm(pl3, Jm, G[:, hf:6:2, w:w + 128], False, j < 2, [CONSTb, TABb], [PSb[li]])
                            if j >= 2:
                                mm(PS[li][:, :384], nm[:, s0:s0 + 128], id3, False, True, [nmb, CONSTb], [PSb[li]])

                        def E():
                            li = st["li"]
                            pt, ptb = PT[pk[0] % 4], PTb[pk[0] % 4]
                            pk[0] += 1
                            st["pt"] = (pt, ptb)
                            P.op("act", lambda e: e.activation(out=pt, in_=PS[li][:, :384], func=AF.Exp, scale=0.125),
                                 [PSb[li]], [ptb])

                        def PV():
                            pt, ptb = st["pt"]
                            oi = 5 + hf
                            mm(PS[oi][:, :384], vlhs(hf, VV_EL + 64 + 192 * c), pt, c == 0, c == j, [VVb, ptb], [PSb[oi]])
                            if c == j:
                                attn_finish(oi, 384, Osb[hf], Osbb[hf], Rd[hf], Rdb[hf], hf,
                                            [(128 * cc, OT[rows, cc, t0:t0 + 128], OTb[cc]) for cc in range(3)], fin_bank=2)
                        return [None, L, E, PV]
                    steps.append(mk())
            return steps

        allsteps = []
        idx_bis_pair(0)
        for j in range(NT):
            st_j = att_steps(j)
            if j % 2 == 0 and j + 2 < NT:
                st_j[0][0] = (lambda jj=j + 2: idx_bis_pair(jj))
            allsteps += st_j
        run_pipe(allsteps)
        P.barrier()

    def phase_C(wi):
        proj_fm(wi, 2176, 6, 0)
        P.op("dve", lambda e: e.memset(raw(VV_EL + 64, [[192, 48], [1, 64]]), 1.0), w=[VVb])

        def c_dst(j, ps, psb):
            for a in range(3):
                o_ = raw(VV_EL + 576 * j + 192 * a, [[128, 2], [1, 64]])
                i_ = ps[:, 128 * a:128 * a + 128].rearrange("p (b c) -> p b c", b=2)
                P.op("act", lambda e, o_=o_, i_=i_: e.activation(out=o_, in_=i_, func=AF.Copy), [psb], [VVb])
        proj_tm(wi, 3272, 384, lambda j: (128 * j, 1), NT, c_dst)
        P.barrier()
        if stop == "C1":
            return
        G = view(A_TAB, [6, S])
        load_tab(eg_s.ap()[1].rearrange("p (a b) -> p a b", a=6), G)
        NMT = view(A_WORK, [6, S])
        NMTb = Buf("nmt")
        PT = [view(A_WORK + 24576 + i * 768, [384]) for i in range(3)]
        PTb = [Buf("pt%d" % i) for i in range(3)]
        Osb, Osbb = view(A_WORK + 26880, [384], F32), Buf("osb")
        Rd, Rdb = view(A_WORK + 28416, [384], F32), Buf("rd")
        GPb = Buf("gp")
        MX = view(A_WORK + 29952, [6], F32)
        EQ = view(A_WORK + 30016, [6, 8], F32)
        G2 = view(A_WORK + 30208, [6, 8], F32)
        G3 = view(A_WORK + 30400, [6, 8], F32)
        ck4 = QK[:, 3:6, :].rearrange("p c (n k) -> p c n k", k=256)
        P.op("dve", lambda e: e.tensor_reduce(out=KMf, in_=ck4, axis=AX.X, op=ALU.add), [QKb[3], QKb[4], QKb[5]], [GPb])
        P.op("dve", lambda e: e.memset(KMp, 0.0), [], [GPb])
        for c in range(3):
            for e_ in range(2):
                h = 2 * c + e_
                rows = slice(64 * e_, 64 * e_ + 64)
                P.op("dve", lambda e, rows=rows, c=c, h=h: e.tensor_copy(out=KMp[rows, c, 8 * h:8 * h + 8], in_=KMf[rows, c, :]),
                     [GPb], [GPb])
        E8v = cbv[0:8, CB_E8:CB_E8 + 1024]
        pk = [0]
        if stop == "C2":
            P.barrier()
            return

        def gate_tile(j):
            cur = j // 2
            t0 = 128 * j
            gis = [nextps(), nextps()]
            for e_ in range(2):
                rows = slice(64 * e_, 64 * e_ + 64)
                for c in range(3):
                    mm(PS[gis[e_]][:, :48], QK[rows, c, t0:t0 + 128], KMp[rows, c, :], c == 0, c == 2, [QKb[c], GPb], [PSb[gis[e_]]])
            P.op("dve", lambda e: e.tensor_tensor(out=GSB, in0=PS[gis[0]][:, :48], in1=cfv[:, CF_CM + 48 * cur:CF_CM + 48 * cur + 48],
                                                  op=ALU.add), [PSb[gis[0]], CONSTb], [GPb])
            P.op("dve", lambda e: e.tensor_tensor(out=GSB, in0=GSB, in1=PS[gis[1]][:, :48], op=ALU.add), [PSb[gis[1]], GPb], [GPb])
            if stop == "C3a":
                return
            g3v = GSB.rearrange("p (a b) -> p a b", a=6)
            cur_src = g3v
            for it in range(2):
                P.op("dve", lambda e, cur_src=cur_src: e.tensor_reduce(out=MX, in_=cur_src, axis=AX.X, op=ALU.max), [GPb], [GPb])
                P.op("dve", lambda e, cur_src=cur_src: e.tensor_tensor(out=EQ, in0=cur_src, in1=MX.unsqueeze(2).to_broadcast([128, 6, 8]),
                                                                        op=ALU.is_ge), [GPb], [GPb])
                dst = G2 if it == 0 else G3
                P.op("dve", lambda e, cur_src=cur_src, dst=dst: e.scalar_tensor_tensor(out=dst, in0=EQ, scalar=-3e30, in1=cur_src,
                                                                                       op0=ALU.mult, op1=ALU.add), [GPb], [GPb])
                cur_src = dst
            P.op("dve", lambda e: e.tensor_reduce(out=MX, in_=G3, axis=AX.X, op=ALU.max), [GPb], [GPb])
            P.op("dve", lambda e: e.tensor_tensor(out=EQ, in0=g3v, in1=MX.unsqueeze(2).to_broadcast([128, 6, 8]), op=ALU.is_lt),
                 [GPb], [GPb])
            P.op("dve", lambda e: e.tensor_scalar(out=NSEL, in0=EQ, scalar1=-BIG, scalar2=None, op0=ALU.mult), [GPb], [GPb])
            if stop == "C3b":
                return
            for h3 in range(2):
                ti = nextps()
                for hh in range(3):
                    h = 3 * h3 + hh
                    mm(PS[ti][0:8, hh * 128:(hh + 1) * 128], NSEL[:, h, :], ident, True, True, [GPb, CONSTb], [PSb[ti]])
                P.op("act", lambda e, ti=ti, h3=h3: e.activation(
                    out=NMT[0:8, 3 * h3:3 * h3 + 3, t0:t0 + 128],
                    in_=PS[ti][0:8, :384].rearrange("p (a b) -> p a b", a=3), func=AF.Copy), [PSb[ti]], [NMTb])

        lk = [0]

        def c_steps(j):
            t0 = 128 * j
            steps = []
            for c in range(j + 1):
                for hf in range(2):
                    def mk(c=c, hf=hf):
                        s0 = 128 * c
                        w = t0 - s0
                        n = c // 2
                        rows = slice(64 * hf, 64 * hf + 64)
                        st = {}

                        def L():
                            li = lk[0] % 4
                            lk[0] += 1
                            st["li"] = li
                            pl3 = PS[li][:, :384].rearrange("p (a b) -> p a b", a=3)
                            own = (n == j // 2)
                            for cc in range(3):
                                mmg(PS[li][:, cc * 128:(cc + 1) * 128], QK[rows, 3 + cc, s0:s0 + 128], QK[rows, cc, t0:t0 + 128],
                                    cc == 0, own and cc == 2, [QKb[cc], QKb[3 + cc]], [PSb[li]])
                            if not own:
                                mmg(pl3, E8v[:, n * 128:(n + 1) * 128], NMT[0:8, hf:6:2, t0:t0 + 128], False, True,
                                    [CONSTb, NMTb], [PSb[li]])

                        def E():
                            li = st["li"]
                            pt, ptb = PT[pk[0] % 3], PTb[pk[0] % 3]
                            pk[0] += 1
                            st["pt"] = (pt, ptb)
                            P.op("act", lambda e: e.activation(out=pt, in_=PS[li][:, :384], func=AF.Exp, scale=0.125),
                                 [PSb[li]], [ptb])
                            pt3 = pt.rearrange("p (a b) -> p a b", a=3)
                            P.op("dve", lambda e: e.tensor_tensor(out=pt3, in0=pt3, in1=G[:, hf:6:2, w:w + 128], op=ALU.mult),
                                 [ptb, TABb], [ptb])

                        def PV():
                            pt, ptb = st["pt"]
                            oi = 4 + hf
                            for cc in range(3):
                                vcol = VV_EL + 576 * c + 192 * cc + (64 if hf else 0)
                                mmg(PS[oi][:, cc * 128:(cc + 1) * 128], raw(vcol, [[1, 128]]), pt[:, cc * 128:(cc + 1) * 128],
                                    c == 0 and cc == 0, c == j, [VVb, ptb], [PSb[oi]])
                            if c == j:
                                attn_finish(oi, 384, Osb, Osbb, Rd, Rdb, hf,
                                            [(128 * cc, OT[rows, 5 + cc, t0:t0 + 128], OTb[5 + cc]) for cc in range(3)], fin_bank=6)
                        return [None, L, E, PV]
                    steps.append(mk())
            return steps
        for j in range(NT):
            gate_tile(j)
        if stop in ("C3", "C3a", "C3b"):
            P.barrier()
            return
        allsteps = []
        for j in range(NT):
            allsteps += c_steps(j)
        run_pipe(allsteps)
        P.barrier()

    def phase_B(wi):
        proj_fm(wi, 1152, 8, 0)
        VB = A_VV // 2
        P.op("dve", lambda e: e.memset(raw(VB + 64, [[192, 96], [1, 64]]), 1.0), w=[VVb])
        for g, dil in enumerate((1, 4, 16)):
            def tokf(tile, g=g, dil=dil):
                if g == 0:
                    return (128 * tile, 1)
                if g == 1:
                    return ((tile // 4) + 4 * 128 * (tile % 4), 4)
                return (tile, 16)

            def b_dst(tile, ps, psb, g=g):
                for a in range(2):
                    o_ = raw(VB + (g * 16 + tile) * 384 + 192 * a, [[128, 2], [1, 64]])
                    i_ = ps[:, 128 * a:128 * a + 128].rearrange("p (b c) -> p b c", b=2)
                    P.op("act", lambda e, o_=o_, i_=i_: e.activation(out=o_, in_=i_, func=AF.Copy), [psb], [VVb])
            proj_tm(wi, 3016, 256, tokf, 16, b_dst)
        P.barrier()
        if stop == "B1":
            return
        GB = view(A_VV + 36864, [12, 256])
        for g in range(3):
            P.dma(st_t, GB[:, 4 * g:4 * g + 4, :],
                  bass.AP(frowb_s, g * 24 * FWB + (6 + 4 * g) * FWB, [[1, 128], [FWB, 4], [1, 256]]),
                  r=[FRb], w=[TABb], new_batch=(g == 0))
        ACC = view(A_WORK, [2, S], F32)
        ACCb = Buf("acc")
        PT = [view(A_WORK + 16384 + i * 512, [256]) for i in range(4)]
        PTb = [Buf("pt%d" % i) for i in range(4)]
        Rd, Rdb = view(A_WORK + 18432, [512], F32), Buf("rd")
        pk = [0]

        lk = [0]

        def group_steps(sp, g, dil, r, q):
            st_q = r + dil * 128 * q
            tq = slice(st_q, st_q + dil * 127 + 1, dil)
            scs = [q - 1, q] if q >= 1 else [q]
            steps = []
            for e_ in range(2):
                for si_, sc in enumerate(scs):
                    def mk(e_=e_, si_=si_, sc=sc):
                        rows = slice(64 * e_, 64 * e_ + 64)
                        oi = 4 + e_
                        st_s = r + dil * 128 * sc
                        tsl = slice(st_s, st_s + dil * 127 + 1, dil)
                        w = 128 * (q - sc)
                        st = {}

                        def L():
                            li = lk[0] % 4
                            lk[0] += 1
                            st["li"] = li
                            mm(PS[li][:, :128], QK[rows, 6 + sp, tsl], QK[rows, 2 * g + sp, tq], True, False,
                               [QKb[6 + sp], QKb[2 * g + sp]], [PSb[li]])
                            mm(PS[li][:, :128], Jm, GB[:, 4 * g + 2 * sp + e_, w:w + 128], False, True, [CONSTb, TABb], [PSb[li]])

                        def E():
                            li = st["li"]
                            pt, ptb = PT[pk[0] % 4], PTb[pk[0] % 4]
                            pk[0] += 1
                            st["pt"] = (pt, ptb)
                            P.op("act", lambda e: e.activation(out=pt[:, :128], in_=PS[li][:, :128], func=AF.Exp, scale=0.125),
                                 [PSb[li]], [ptb])

                        def PV():
                            pt, ptb = st["pt"]
                            tile = sc if g == 0 else (r * 4 + sc if g == 1 else r)
                            vcol = VB + (g * 16 + tile) * 384 + 192 * sp + (64 if e_ else 0)
                            mm(PS[oi][:, :128], raw(vcol, [[1, 128]]), pt[:, :128], si_ == 0, si_ == len(scs) - 1,
                               [VVb, ptb], [PSb[oi]])
                            if si_ == len(scs) - 1:
                                accv = ACC[:, e_, tq]
                                if g == 0:
                                    P.op("dve", lambda e: e.tensor_copy(out=accv, in_=PS[oi][:, :128]), [PSb[oi]], [ACCb])
                                else:
                                    P.op("dve", lambda e: e.tensor_tensor(out=accv, in0=accv, in1=PS[oi][:, :128], op=ALU.add),
                                         [PSb[oi], ACCb], [ACCb])
                        return [None, L, E, PV]
                    steps.append(mk())
            return steps

        for sp in range(2):
            allsteps = []
            for g, dil in enumerate((1, 4, 16)):
                for r in range(dil):
                    for q in range(16 // dil):
                        allsteps += group_steps(sp, g, dil, r, q)
            run_pipe(allsteps)
            for e_ in range(2):
                rows = slice(64 * e_, 64 * e_ + 64)
                for tb in range(4):
                    def fin(e_=e_, rows=rows, tb=tb, sp=sp):
                        i = nextps()
                        mm(PS[i], swapm, ACC[:, e_, tb * 512:(tb + 1) * 512], True, True, [CONSTb, ACCb], [PSb[i]])
                        P.op("dve", lambda e: e.reciprocal(out=Rd[rows, :], in_=PS[i][rows, :]), [PSb[i]], [Rdb])
                        P.op("dve", lambda e: e.tensor_tensor(out=OT[rows, 3 + sp, tb * 512:(tb + 1) * 512],
                                                              in0=ACC[rows, e_, tb * 512:(tb + 1) * 512], in1=Rd[rows, :],
                                                              op=ALU.mult), [ACCb, Rdb], [OTb[3 + sp]])
                    fin()
        P.barrier()

    def load_ln(dst, src_h, L, rowbuf, rowb, LNb):
        P.dma(st_m, rowbuf[0:1, :], src_h.ap()[L:L + 1, :], w=[rowb])
        for hh in range(2):
            i = nextps()
            mm(PS[i], ONESf[0:1, :], rowbuf[0:1, hh * 512:(hh + 1) * 512], True, True, [CONSTb, rowb], [PSb[i]])
            P.op("dve", lambda e, i=i, hh=hh: e.tensor_copy(out=dst[:, hh * 512:(hh + 1) * 512], in_=PS[i]), [PSb[i]], [LNb])

    STbs = [Buf("st0"), Buf("st1")]

    def ln_ops(xap, LG, LB, PARb, junk, xbuf, sset):
        o0 = 8 * sset
        stb = STbs[sset]
        s1, nmean, s2, rstd = [STv[:, o0 + q:o0 + q + 1] for q in range(4)]
        ops = []
        ops.append(lambda: P.op("dve", lambda e: e.scalar_tensor_tensor(out=junk, in0=xap, scalar=1.0, in1=zcol.to_broadcast([128, D]),
                                                                      op0=ALU.mult, op1=ALU.add, accum_out=s1), [xbuf, CONSTb], [stb]))
        ops.append(lambda: P.op("dve", lambda e: e.tensor_scalar(out=nmean, in0=s1, scalar1=-1.0 / D, scalar2=None, op0=ALU.mult),
                                [stb], [stb]))
        ops.append(lambda: P.op("dve", lambda e: e.tensor_scalar(out=xap, in0=xap, scalar1=nmean, scalar2=None, op0=ALU.add),
                                [xbuf, stb], [xbuf]))
        ops.append(lambda: P.op("dve", lambda e: e.scalar_tensor_tensor(out=junk, in0=xap, scalar=1.0, in1=xap,
                                                                      op0=ALU.mult, op1=ALU.mult, accum_out=s2), [xbuf], [stb]))
        ops.append(lambda: P.op("dve", lambda e: e.tensor_scalar(out=rstd, in0=s2, scalar1=1.0 / D, scalar2=EPS, op0=ALU.mult,
                                                                  op1=ALU.add), [stb], [stb]))
        ops.append(lambda: P.op("act", lambda e: e.activation(out=rstd, in_=rstd, func=AF.Sqrt), [stb], [stb]))
        ops.append(lambda: P.op("dve", lambda e: e.reciprocal(out=rstd, in_=rstd), [stb], [stb]))
        ops.append(lambda: P.op("dve", lambda e: e.scalar_tensor_tensor(out=xap, in0=xap, scalar=rstd, in1=LG, op0=ALU.mult,
                                                                      op1=ALU.mult), [xbuf, stb, PARb], [xbuf]))
        ops.append(lambda: P.op("dve", lambda e: e.tensor_tensor(out=xap, in0=xap, in1=LB, op=ALU.add), [xbuf, PARb], [xbuf]))
        return ops

    def zip_run(lists):
        n = max(len(l) for l in lists)
        for i in range(n):
            for l in lists:
                if i < len(l):
                    l[i]()

    def phase_M(L):
        MT = view(A_QK, [8, S])
        MTb = [Buf("mt%d" % c) for c in range(8)]
        SG = [view(A_WORK + i * 2048, [512], F32) for i in range(2)]
        SGb = [Buf("sg%d" % i) for i in range(2)]
        ACm, ACmb = view(A_WORK + 4096, [512], F32), Buf("acm")
        wg2 = wg_s.ap()[L].rearrange("(k p) c -> p k c", p=128)
        wb2 = wbr_s.ap()[L].rearrange("(k p) c -> p k c", p=128)
        chunks = {0: [0, 1, 2], 1: [3, 4], 2: [5, 6, 7]}
        sk = [0]

        def do_cc(cc):
            k = wrr[0] % 2
            wrr[0] += 1
            wgv = view(A_WR + k * 8192, [3, 8, 128])
            pbv = view(A_WR + k * 8192 + 6144, [8, 128])
            for b_ in range(3):
                P.dma(st_w[k], wgv[:, b_, :, :], wg2[:, :, b_ * 1024 + cc * 128:b_ * 1024 + cc * 128 + 128],
                      r=[WSb], w=[WRb[k]], new_batch=(b_ == 0))
            P.dma(st_w[k], pbv, wb2[:, :, cc * 128:cc * 128 + 128], r=[WSb], w=[WRb[k]], new_batch=False)

            def do_tb(tb):
                cols = slice(tb * 512, (tb + 1) * 512)
                for b_ in range(3):
                    io = nextps()
                    ch = chunks[b_]
                    for n_, kk in enumerate(ch):
                        mm(PS[io], pbv[:, kk, :], OT[:, kk, cols], n_ == 0, n_ == len(ch) - 1, [WRb[k], OTb[kk]], [PSb[io]])
                    ig = nextps()
                    for kc in range(8):
                        mm(PS[ig], wgv[:, b_, kc, :], XT[:, kc, cols], kc == 0, kc == 7, [WRb[k]] + XTb[4 * tb:4 * tb + 4], [PSb[ig]])
                    sg, sgb = SG[sk[0] % 2], SGb[sk[0] % 2]
                    sk[0] += 1
                    P.op("act", lambda e, sg=sg, ig=ig: e.activation(out=sg, in_=PS[ig], func=AF.Sigmoid), [PSb[ig]], [sgb])
                    if b_ == 0:
                        P.op("dve", lambda e, sg=sg, io=io: e.tensor_tensor(out=ACm, in0=sg, in1=PS[io], op=ALU.mult),
                             [sgb, PSb[io]], [ACmb])
                    else:
                        P.op("dve", lambda e, sg=sg, io=io: e.tensor_tensor(out=sg, in0=sg, in1=PS[io], op=ALU.mult),
                             [sgb, PSb[io]], [sgb])
                        if b_ == 1:
                            P.op("dve", lambda e, sg=sg: e.tensor_tensor(out=ACm, in0=ACm, in1=sg, op=ALU.add), [sgb, ACmb], [ACmb])
                        else:
                            P.op("dve", lambda e, sg=sg: e.tensor_tensor(out=MT[:, cc, cols], in0=ACm, in1=sg, op=ALU.add),
                                 [sgb, ACmb], [MTb[cc]])
            for tb in range(4):
                do_tb(tb)
        for cc in range(8):
            do_cc(cc)
        P.barrier()
        return MT, MTb

    def phase_R1(b, L, MT, MTb):
        X1 = view(A_VV, [NT, D], F32)
        X1b = [Buf("x1_%d" % j) for j in range(NT)]
        LG, LB = view(A_OT, [D], F32), view(A_OT + 4096, [D], F32)
        LNb = Buf("ln")
        XS = [view(A_OT + 8192 + i * 4096, [D], F32) for i in range(2)]
        XSb = [Buf("xs%d" % i) for i in range(2)]
        XB = [view(A_OT + 16384 + i * 2048, [D]) for i in range(2)]
        XBb = [Buf("xb%d" % i) for i in range(2)]
        JK = view(A_OT + 20480, [D], F32)
        rowb = Buf("lnrow")
        load_ln(LG, ln1g_h, L, JK, rowb, LNb)
        load_ln(LB, ln1b_h, L, JK, rowb, LNb)
        wo2 = wo_s.ap()[L].rearrange("(k p) c -> p k c", p=128)
        for hh in range(2):
            P.dma(st_w[hh], WR[hh], wo2[:, :, hh * 512:(hh + 1) * 512], r=[WSb], w=[WRb[hh]])

        JK2 = [JK, view(A_OT + 24576, [D], F32)]

        def stage1(j):
            t0 = 128 * j
            k = j % 2
            src = x_h.ap()[b, t0:t0 + 128, :] if L == 0 else xres_s.ap()[t0:t0 + 128, :]
            P.dma(st_x[k], XS[k], src, r=[XRb], w=[XSb[k]])
            for hh in range(2):
                i = nextps()
                for cc in range(8):
                    mm(PS[i], MT[:, cc, t0:t0 + 128], WR[hh][:, cc, :], cc == 0, cc == 7, [MTb[cc], WRb[hh]], [PSb[i]])
                P.op("dve", lambda e, i=i, hh=hh: e.scalar_tensor_tensor(
                    out=X1[:, j, hh * 512:(hh + 1) * 512], in0=XS[k][:, hh * 512:(hh + 1) * 512], scalar=ALPHA,
                    in1=PS[i], op0=ALU.mult, op1=ALU.add), [XSb[k], PSb[i]], [X1b[j]])

        def stage3(j):
            k = j % 2
            P.op("act", lambda e: e.activation(out=XB[k], in_=X1[:, j, :], func=AF.Copy), [X1b[j]], [XBb[k]])
            to_xt(j, XB[k], XBb[k])
            P.op("act", lambda e: e.activation(out=X1[:, j, :], in_=X1[:, j, :], func=AF.Copy, scale=ALPHA),
                 [X1b[j], XBb[k]], [X1b[j]])

        def do_pair(j0):
            stage1(j0)
            stage1(j0 + 1)
            zip_run([ln_ops(X1[:, j0 + q, :], LG, LB, LNb, JK2[q], X1b[j0 + q], q) for q in range(2)])
            stage3(j0)
            stage3(j0 + 1)
        for j0 in range(0, NT, 2):
            do_pair(j0)
        P.barrier()
        return X1, X1b

    def phase_F(b, L, last, X1, X1b):
        HT = view(A_QK, [8, S])
        HTb = [Buf("ht%d" % c) for c in range(8)]
        W4 = [view(A_WR, [8, 512]), view(A_WR + 8192, [8, 512]),
              view(A_WORK + 16384, [8, 512]), view(A_WORK + 24576, [8, 512])]
        W4b = [Buf("w4_%d" % i) for i in range(4)]
        w4r = [0]
        TM = [view(A_OT + i * 2048, [512], F32) for i in range(2)]
        TMb = [Buf("tm%d" % i) for i in range(2)]
        tk = [0]
        wu2 = wu_s.ap()[L].rearrange("(k p) c -> p k c", p=128)

        def wload(srcv):
            k = w4r[0] % 4
            w4r[0] += 1
            P.dma(st_w[k], W4[k], srcv, r=[WSb], w=[W4b[k]])
            return W4[k], W4b[k]

        def up_half(g, half):
            wt, wb = wload(wu2[:, :, g * 1024 + half * 512:g * 1024 + half * 512 + 512])
            for hc in range(4):
                for tb in range(4):
                    def one(hc=hc, tb=tb):
                        cols = slice(tb * 512, (tb + 1) * 512)
                        i = nextps()
                        for kc in range(8):
                            mm(PS[i], wt[:, kc, hc * 128:(hc + 1) * 128], XT[:, kc, cols], kc == 0, kc == 7,
                               [wb] + XTb[4 * tb:4 * tb + 4], [PSb[i]])
                        tm, tmb = TM[tk[0] % 2], TMb[tk[0] % 2]
                        tk[0] += 1
                        P.op("act", lambda e: e.activation(out=tm, in_=PS[i], func=AF.Relu), [PSb[i]], [tmb])
                        eng = "dve"
                        P.op(eng, lambda e: e.tensor_tensor(out=HT[:, half * 4 + hc, cols], in0=tm, in1=tm, op=ALU.mult),
                             [tmb], [HTb[half * 4 + hc]])
                    one()

        def down_half(g, h2):
            wd2 = wd_s.ap()[L][g * 1024:(g + 1) * 1024, :].rearrange("(k p) c -> p k c", p=128)
            wt, wb = wload(wd2[:, :, h2 * 512:(h2 + 1) * 512])
            for j in range(NT):
                def one(j=j):
                    i = nextps()
                    for hc in range(8):
                        mm(PS[i], HT[:, hc, j * 128:(j + 1) * 128], wt[:, hc, :], hc == 0, hc == 7, [HTb[hc], wb], [PSb[i]])
                    P.op("dve", lambda e: e.tensor_tensor(out=X1[:, j, h2 * 512:(h2 + 1) * 512],
                                                          in0=X1[:, j, h2 * 512:(h2 + 1) * 512], in1=PS[i], op=ALU.add),
                         [X1b[j], PSb[i]], [X1b[j]])
                one()

        for g in range(4):
            for half in range(2):
                up_half(g, half)
            for h2 in range(2):
                down_half(g, h2)
        P.barrier()
        wpg2 = wpg_s.ap()[L].rearrange("(k p) c -> p k c", p=128)
        WPG = []
        for hh in range(2):
            WPG.append(wload(wpg2[:, :, hh * 512:(hh + 1) * 512]))
        WPL = view(A_QK, [2, D])
        WPLb = Buf("wpl")
        P.dma(st_m, WPL, wpl_s.ap()[L].rearrange("(k p) c -> p k c", p=128), r=[WSb], w=[WPLb])
        LG, LB = view(A_QK + 4096, [D], F32), view(A_QK + 8192, [D], F32)
        LNb = Buf("ln2")
        JK = view(A_QK + 12288, [D], F32)
        rowb = Buf("lnrow2")
        load_ln(LG, ln2g_h, L, JK, rowb, LNb)
        load_ln(LB, ln2b_h, L, JK, rowb, LNb)
        PSg = [view(A_QK + 16384 + i * 1024, [256], F32) for i in range(2)]
        PSgb = [Buf("pf%d" % i) for i in range(2)]
        PBf = [view(A_QK + 18432 + i * 512, [256]) for i in range(2)]
        PBfb = [Buf("pb%d" % i) for i in range(2)]
        PTt = [view(A_QK + 19456 + i * 512, [2, 128]) for i in range(2)]
        PTtb = [Buf("ptt%d" % i) for i in range(2)]
        SG = [view(A_QK + 20480 + i * 2048, [512], F32) for i in range(2)]
        SGb = [Buf("sg%d" % i) for i in range(2)]
        XB = [view(A_QK + 24576 + i * 2048, [D]) for i in range(2)]
        XBb = [Buf("xb%d" % i) for i in range(2)]
        sk = [0]

        JK2 = [JK, view(A_QK + 28672, [D], F32)]

        def stage1(j):
            t0 = 128 * j
            k = j % 2
            P.dma(st_x[k], PSg[k], p_h.ap()[L, b, t0:t0 + 128, :], w=[PSgb[k]])
            P.op("pool", lambda e: e.tensor_copy(out=PBf[k], in_=PSg[k]), [PSgb[k]], [PBfb[k]])
            i = nextps()
            pbf = PS[i].bitcast(BF16)
            for c2 in range(2):
                P.op("pe", lambda e, c2=c2: e.transpose(pbf[:, c2 * 128:(c2 + 1) * 128], PBf[k][:, c2 * 128:(c2 + 1) * 128], ident),
                     [PBfb[k], CONSTb], [PSb[i]])
            P.op("act", lambda e: e.activation(out=PTt[k], in_=pbf[:, 0:256].rearrange("p (a b) -> p a b", a=2), func=AF.Copy),
                 [PSb[i]], [PTtb[k]])
            for hh in range(2):
                def one(hh=hh):
                    cols = slice(hh * 512, (hh + 1) * 512)
                    ig = nextps()
                    wt, wb = WPG[hh]
                    for kc in range(8):
                        mm(PS[ig], XT[:, kc, t0:t0 + 128], wt[:, kc, :], kc == 0, kc == 7, [XTb[j], wb], [PSb[ig]])
                    ip = nextps()
                    for c2 in range(2):
                        mm(PS[ip], PTt[k][:, c2, :], WPL[:, c2, cols], c2 == 0, c2 == 1, [PTtb[k], WPLb], [PSb[ip]])
                    sg, sgb = SG[sk[0] % 2], SGb[sk[0] % 2]
                    sk[0] += 1
                    P.op("act", lambda e: e.activation(out=sg, in_=PS[ig], func=AF.Sigmoid), [PSb[ig]], [sgb])
                    P.op("dve", lambda e: e.tensor_tensor(out=sg, in0=sg, in1=PS[ip], op=ALU.mult), [sgb, PSb[ip]], [sgb])
                    P.op("dve", lambda e: e.tensor_tensor(out=X1[:, j, cols], in0=X1[:, j, cols], in1=sg, op=ALU.add),
                         [sgb, X1b[j]], [X1b[j]])
                one()

        def stage3(j):
            t0 = 128 * j
            k = j % 2
            if last:
                P.dma(st_o[k], out_h.ap()[b, t0:t0 + 128, :], X1[:, j, :], r=[X1b[j]])
            else:
                P.dma(st_o[k], xres_s.ap()[t0:t0 + 128, :], X1[:, j, :], r=[X1b[j]], w=[XRb])
                P.op("act", lambda e: e.activation(out=XB[k], in_=X1[:, j, :], func=AF.Copy), [X1b[j]], [XBb[k]])
                to_xt(j, XB[k], XBb[k])

        def do_pair(j0):
            stage1(j0)
            stage1(j0 + 1)
            zip_run([ln_ops(X1[:, j0 + q, :], LG, LB, LNb, JK2[q], X1b[j0 + q], q) for q in range(2)])
            stage3(j0)
            stage3(j0 + 1)
        for j0 in range(0, NT, 2):
            do_pair(j0)
        P.barrier()

    XRb = Buf("xres")

    def layer(b, L, last):
        wi = wi_s.ap()[L]
        phase_A(wi)
        if stop == "A":
            return
        phase_C(wi)
        if stop in ("C", "C1", "C2", "C3", "C3a", "C3b"):
            return
        phase_B(wi)
        if stop in ("B", "B1"):
            return
        MT, MTb = phase_M(L)
        if stop == "M":
            return
        X1, X1b = phase_R1(b, L, MT, MTb)
        if "x1" in dbg_h:
            for j in range(NT):
                P.dma(st_m, dbg_h["x1"].ap()[j * 128:(j + 1) * 128, :], X1[:, j, :], r=[X1b[j]], new_batch=(j == 0))
            P.barrier()
        if stop == "R1":
            return
        phase_F(b, L, last, X1, X1b)

    for b in range(NB):
        def xsrc(j, b=b):
            return x_h.ap()[b, j * 128:(j + 1) * 128, :]
        load_x_tiles(xsrc, 0)
        P.barrier()
        for L in range(NL):
            layer(b, L, L == NL - 1)
            if stop is not None:
                break
            P.new_epoch()
        if stop is not None:
            break
    if "ot_all" in dbg_h:
        P.dma(st_m, dbg_h["ot_all"].ap(), OT, r=OTb)
    P.barrier()
    P.emit()
    return nc


def kernel(**inputs):
    NCORE, NB = 8, 4
    cb, cf, oh, ohb = make_consts()
    nc = build(NB, 2)
    names = ["w_in", "w_gate", "w_br_a", "w_br_b", "w_br_c", "w_out", "w_up", "w_down", "w_ple_gate",
             "w_ple", "ln1_g", "ln1_b", "ln2_g", "ln2_b", "rel_bias"]
    shared = {k: np.ascontiguousarray(np.asarray(inputs[k], dtype=np.float32)) for k in names}
    shared.update({"c_bf": cb, "c_f32": cf, "c_oh": oh, "c_ohb": ohb})
    x = np.asarray(inputs["x"], dtype=np.float32)
    p = np.asarray(inputs["p"], dtype=np.float32)
    in_maps = []
    for c in range(NCORE):
        m = dict(shared)
        m["x"] = np.ascontiguousarray(x[c * NB:(c + 1) * NB])
        m["p"] = np.ascontiguousarray(p[:, c * NB:(c + 1) * NB])
        in_maps.append(m)
    res = run_bass_kernel_spmd(nc, in_maps, core_ids=list(range(NCORE)))
    return np.concatenate([np.asarray(r["out"], dtype=np.float32) for r in res.results], axis=0)
```
